# Optimizing a Trainium2 kernel written in Bass

```python
import jax, jax.numpy as jnp
from jax import lax
import numpy as np

D_MODEL = 1024
BATCH = 8
SEQ = 2048
DEPTH = 2

CHUNK = 64
PLE_DIM = 256
N_A = DEPTH // 2
N_B = DEPTH - N_A
POOL_WINDOWS = (2, 4, 8, 16)
N_POOL_GROUPS = len(POOL_WINDOWS)
POOL_GROUP_DIM = D_MODEL // N_POOL_GROUPS
POOL_WMAX = max(POOL_WINDOWS)
SB_HEADS = 16
SB_HEAD_DIM = D_MODEL // SB_HEADS
SB_SCALE = SB_HEAD_DIM ** -0.5
Q_BLOCK = 128
EPS = 1e-6

kernel_name = "yoco_pool_stickbreaking_hybrid"


def rms_norm(x, g):
    xf = x.astype(jnp.float32)
    y = xf * lax.rsqrt(jnp.mean(xf * xf, axis=-1, keepdims=True) + EPS)
    return (y * g.astype(jnp.float32)).astype(x.dtype)


def pool_mixer(h, w_in, w_group, scale, w_out):
    B, S, _ = h.shape
    u, z = jnp.split(h @ w_in, 2, axis=-1)
    u = u.reshape(B, S, N_POOL_GROUPS, POOL_GROUP_DIM)
    uf = u.astype(jnp.float32)
    cs = jnp.cumsum(uf, axis=1)
    cs_ext = jnp.pad(cs, ((0, 0), (POOL_WMAX, 0), (0, 0), (0, 0)))
    pos1 = jnp.arange(1, S + 1)
    means = []
    for g, w in enumerate(POOL_WINDOWS):
        lo = cs_ext[:, POOL_WMAX - w: POOL_WMAX - w + S, g]
        cnt = jnp.minimum(pos1, w).astype(jnp.float32)[None, :, None]
        means.append((cs[:, :, g] - lo) / cnt)
    pooled = (jnp.stack(means, axis=2) - uf).astype(h.dtype)
    mixed = jnp.einsum('bsgc,gcd->bsgd', pooled, w_group).reshape(B, S, D_MODEL) * scale
    return (mixed * jax.nn.silu(z)) @ w_out


def split_heads(t):
    B, S, _ = t.shape
    return t.reshape(B, S, SB_HEADS, SB_HEAD_DIM).transpose(0, 2, 1, 3)


def stick_breaking_attention(q, k, v):
    S = q.shape[2]
    outs = []
    for blk in range(S // Q_BLOCK):
        t0 = blk * Q_BLOCK
        L = t0 + Q_BLOCK
        logits = jnp.einsum('bhtd,bhsd->bhts', q[:, :, t0:L], k[:, :, :L]).astype(jnp.float32) * SB_SCALE
        t_idx = t0 + jnp.arange(Q_BLOCK)[:, None]
        s_idx = jnp.arange(L)[None, :]
        mask = s_idx < t_idx
        log_keep = jnp.where(mask, jax.nn.log_sigmoid(-logits), 0.0)
        later = lax.cumsum(log_keep, axis=log_keep.ndim - 1, reverse=True) - log_keep
        log_a = jax.nn.log_sigmoid(logits) + later
        a = jnp.where(mask, jnp.exp(log_a), 0.0)
        outs.append(jnp.einsum('bhts,bhsd->bhtd', a.astype(v.dtype), v[:, :, :L]))
    return jnp.concatenate(outs, axis=2)


def setup_inputs(seed: int = 0) -> dict:
    key = jax.random.key(seed)
    ks = jax.random.split(key, 20)
    D, C = D_MODEL, POOL_GROUP_DIM
    f32 = jnp.float32

    def nrm(k, shape, fan_in, gain=1.0):
        return jax.random.normal(k, shape, f32) * (gain * fan_in ** -0.5)

    def gain(k, shape):
        return 1.0 + 0.05 * jax.random.normal(k, shape, f32)

    return {
        "x": jax.random.normal(ks[0], (BATCH, SEQ, D), f32),
        "p": jax.random.normal(ks[1], (DEPTH, BATCH, SEQ, PLE_DIM), f32),
        "a_norm": gain(ks[2], (N_A, D)),
        "a_w_in": nrm(ks[3], (N_A, D, 2 * D), D),
        "a_w_group": nrm(ks[4], (N_A, N_POOL_GROUPS, C, C), C),
        "a_scale": gain(ks[5], (N_A, D)),
        "a_w_out": nrm(ks[6], (N_A, D, D), D, 0.5),
        "kv_norm": gain(ks[7], (D,)),
        "w_kv": nrm(ks[8], (D, 2 * D), D),
        "k_norm": gain(ks[9], (SB_HEAD_DIM,)),
        "b_norm": gain(ks[10], (N_B, D)),
        "b_w_in": nrm(ks[11], (N_B, D, 2 * D), D),
        "b_q_norm": gain(ks[12], (N_B, SB_HEAD_DIM)),
        "b_w_out": nrm(ks[13], (N_B, D, D), D, 0.5),
        "ple_w": nrm(ks[14], (DEPTH, PLE_DIM, D), PLE_DIM, 0.5),
        "ple_gate_w": nrm(ks[15], (DEPTH, D, D), D),
    }


def reference(x, p, a_norm, a_w_in, a_w_group, a_scale, a_w_out, kv_norm, w_kv, k_norm,
              b_norm, b_w_in, b_q_norm, b_w_out, ple_w, ple_gate_w):
    B, S, _ = x.shape
    k_sh = v_sh = None
    for i in range(DEPTH):
        if i < N_A:
            h = rms_norm(x, a_norm[i])
            x = x + pool_mixer(h, a_w_in[i], a_w_group[i], a_scale[i], a_w_out[i])
        else:
            j = i - N_A
            if j == 0:
                kv_in = rms_norm(x, kv_norm)
                k_all, v_all = jnp.split(kv_in @ w_kv, 2, axis=-1)
                k_sh = rms_norm(split_heads(k_all), k_norm)
                v_sh = split_heads(v_all)
            h = rms_norm(x, b_norm[j])
            q, z = jnp.split(h @ b_w_in[j], 2, axis=-1)
            q = rms_norm(split_heads(q), b_q_norm[j])
            o = stick_breaking_attention(q, k_sh, v_sh)
            o = o.transpose(0, 2, 1, 3).reshape(B, S, D_MODEL)
            x = x + (o * jax.nn.silu(z)) @ b_w_out[j]
        x = x + (p[i] @ ple_w[i]) * jax.nn.sigmoid(x @ ple_gate_w[i])
    return x
```

```python
import numpy as np
import concourse.bass as bass
import concourse.mybir as mybir
from concourse.bass_utils import run_bass_kernel_spmd

F32 = mybir.dt.float32
BF16 = mybir.dt.bfloat16
AF = mybir.ActivationFunctionType
ALU = mybir.AluOpType

D = 1024
S = 2048
NC_ = 8
TT = 512
NTT = S // TT
PLE = 256
EPS = 1e-6
BIG = 30000.0
SB_SCALE = 64 ** -0.5
POOL_W = (2, 4, 8, 16)
DBG = {'nchunks': 8, 'ntiles': None, 'stage': 9, 'dummy': 1}

V_ANORM, V_ASCALE, V_KVN, V_BN, V_KN, V_QN = 0, 8, 16, 24, 32, 33
NVEC = 34
C_ID, C_TRI, C_BLK, C_OND, C_NEGP, C_NEGN, C_ONE = range(7)
NCB = 7


class Prog:
    CHUNK = 4000
    NDMA = 20

    def __init__(self, nc, sems):
        self.nc = nc
        self.engs = {'pe': nc.tensor, 'act': nc.scalar, 'dve': nc.vector,
                     'pool': nc.gpsimd, 'sp': nc.sync}
        self.free_sems = list(sems)
        self.esem = {e: [] for e in self.engs}
        self.ecount = {e: 0 for e in self.engs}
        self.dsem = [self.free_sems.pop() for _ in range(self.NDMA)]
        self.dcount = [0] * self.NDMA
        self.dlast = [None] * self.NDMA
        self.dnext = 0
        self.known = {e: {} for e in self.engs}
        self.last_w = {}
        self.readers = {}
        self.uid = 0
        self.ops = []
        self.events = {}
        self.phase_start = 0
        self.barrier_events = []
        self.n_inst = 0

    def op(self, eng, fn, reads=(), writes=(), dma=False):
        uid = self.uid
        self.uid += 1
        writes = tuple(writes) + tuple(k for k in reads if k.startswith('ps') and k not in writes)
        reads = tuple(k for k in reads if not k.startswith('ps'))
        deps = set()
        for k in reads:
            w = self.last_w.get(k)
            if w is not None:
                deps.add(w)
        for k in writes:
            w = self.last_w.get(k)
            if w is not None:
                deps.add(w)
            for r in self.readers.get(k, ()):
                deps.add(r)
        for k in reads:
            self.readers.setdefault(k, []).append(uid)
        for k in writes:
            self.last_w[k] = uid
            self.readers[k] = []
        deps.discard(uid)
        self.ops.append(dict(uid=uid, eng=eng, fn=fn, dma=dma, deps=deps))
        return uid

    def _wait(self, eng, ev):
        sem, val = ev
        kn = self.known[eng]
        if kn.get(id(sem), 0) >= val:
            return
        self.engs[eng].wait_ge(sem, val)
        kn[id(sem)] = val
        self.n_inst += 1

    def emit(self):
        ops = self.ops
        self.ops = []
        info = {o['uid']: o for o in ops}
        ps = self.phase_start
        signaling = set()
        for o in ops:
            best = {}
            dmas = []
            for d in o['deps']:
                if d < ps:
                    continue
                p = info[d]
                if p['dma']:
                    dmas.append(d)
                    continue
                if p['eng'] == 'pe' and o['eng'] == 'pe' and not o['dma']:
                    continue
                if d > best.get(p['eng'], -1):
                    best[p['eng']] = d
            o['wdeps'] = sorted(best.values()) + sorted(dmas)
            for d in best.values():
                signaling.add(d)
        last_on_eng = {}
        for o in ops:
            if not o['dma']:
                last_on_eng[o['eng']] = o['uid']
        for u in last_on_eng.values():
            signaling.add(u)
        first_seen = set()
        dma_events = []
        for o in ops:
            eng = o['eng']
            if eng not in first_seen:
                first_seen.add(eng)
                for ev in self.barrier_events:
                    self._wait(eng, ev)
            for d in o['wdeps']:
                self._wait(eng, self.events[d])
            if o['dma'] and eng == 'pool':
                sem = self.free_sems.pop()
                inst = o['fn'](self.engs[eng])
                inst.then_inc(sem, 16)
                ev = (sem, 16)
                self.events[o['uid']] = ev
                dma_events.append(ev)
            elif o['dma']:
                r = self.dnext
                self.dnext = (self.dnext + 1) % self.NDMA
                if self.dlast[r] is not None:
                    self._wait(eng, self.dlast[r])
                inst = o['fn'](self.engs[eng])
                self.dcount[r] += 16
                inst.then_inc(self.dsem[r], 16)
                ev = (self.dsem[r], self.dcount[r])
                self.dlast[r] = ev
                self.events[o['uid']] = ev
                dma_events.append(ev)
            else:
                inst = o['fn'](self.engs[eng])
                if o['uid'] in signaling:
                    c = self.ecount[eng]
                    k = c // self.CHUNK
                    while len(self.esem[eng]) <= k:
                        self.esem[eng].append(self.free_sems.pop())
                    sem = self.esem[eng][k]
                    inst.then_inc(sem, 1)
                    self.ecount[eng] = c + 1
                    self.events[o['uid']] = (sem, c - k * self.CHUNK + 1)
            self.n_inst += 1
        be = []
        for e, u in last_on_eng.items():
            be.append(self.events[u])
        seen = {}
        for ev in dma_events:
            seen[id(ev[0])] = ev
        be.extend(seen.values())
        self.barrier_events = be + [ev for ev in self.barrier_events
                                    if all(id(ev[0]) != id(b[0]) for b in be)]
        self.phase_start = self.uid

    def final_wait(self, eng='sp'):
        for ev in self.barrier_events:
            self._wait(eng, ev)


def I(name, *a, **k):
    return lambda e: getattr(e, name)(*a, **k)


def SEQ(*fs):
    def fn(e):
        inst = None
        for f in fs:
            inst = f(e)
        return inst
    return fn


def build_program(nc, n_layers=2):
    from contextlib import ExitStack

    def din(name, shape):
        return nc.dram_tensor(name, list(shape), F32, kind="ExternalInput").ap()

    xT_d = din("xT", [D, S])
    pT_d = din("pT", [2, PLE, S])
    vecs_d = din("vecs", [128, NVEC])
    cst_d = nc.dram_tensor("cst", [128, NCB * 128], BF16, kind="ExternalInput").ap()
    rc_d = din("rc", [128, 64])
    awin_d = din("a_w_in", [D, 2 * D])
    awg_d = din("a_w_group", [D, 256])
    awout_d = din("a_w_out", [D, D])
    wkv_d = din("w_kv_s", [16, D, 128])
    bwin_d = din("b_w_in_s", [16, D, 128])
    bwout_d = din("b_w_out", [D, D])
    plew_d = din("ple_w", [2, PLE, D])
    gate_d = din("ple_gate_w", [2, D, D])
    out_d = nc.dram_tensor("outT", [D, S], F32, kind="ExternalOutput").ap()
    dbg_d = nc.dram_tensor("dbgy", [D, S], F32, kind="ExternalOutput").ap() if DBG.get('dump') else None

    es = ExitStack()

    def sb(name, shape, dt):
        return es.enter_context(nc.sbuf_tensor(name, list(shape), dt))

    sems = [es.enter_context(nc.semaphore(f"s{i}")) for i in range(100)]
    P = Prog(nc, sems)

    xT = sb("xT_sb", [128, NC_, S], F32)
    vecs = sb("vecs_sb", [128, NVEC], F32)
    rcf = sb("rcf", [128, 64], F32)
    cstb = sb("cstb", [128, NCB * 128], BF16)
    zeros64 = sb("zeros64", [128, 64], BF16)
    qgain = sb("qgain", [128, 1], F32)
    nkgain = sb("nkgain", [128, 1], F32)
    sqb = [sb(f"sqb{i}", [128, TT], BF16) for i in range(2)]
    lnv = [sb(f"lnv{i}", [128, TT], F32) for i in range(2)]
    rstd = lnv
    ps = es.enter_context(nc.psum_tensor("ps", [128, 8, TT], F32))

    def cb(i):
        return cstb[:, i * 128:(i + 1) * 128]

    ident, tri, blk, ond, negp, negn, onef = [cb(i) for i in range(NCB)]
    rc = rcf[:, :]

    st_i = [0]
    stage = []

    def next_stage():
        r = st_i[0] % len(stage)
        st_i[0] += 1
        return r

    bank_i = [0]

    def next_bank():
        b = bank_i[0]
        bank_i[0] = (b + 1) % 8
        return b

    def xk(c, tt):
        return f'x{c}.{tt}'

    def xap(c, tt):
        return xT[:, c, tt * TT:(tt + 1) * TT]

    P.op('sp', I('dma_start', out=vecs[:], in_=vecs_d), writes=['vecs'], dma=True)
    P.op('sp', I('dma_start', out=cstb[:], in_=cst_d), writes=['cstb'], dma=True)
    P.op('sp', I('dma_start', out=rcf[:], in_=rc_d), writes=['cstf'], dma=True)
    P.op('pool', I('memset', zeros64[:], 0.0), writes=['zeros64'])
    P.op('pool', I('tensor_scalar', qgain[:], vecs[:, V_QN:V_QN + 1], SB_SCALE, 1.0, ALU.mult, ALU.mult),
         reads=['vecs'], writes=['qgain'])
    P.op('pool', I('tensor_scalar', nkgain[:], vecs[:, V_KN:V_KN + 1], -1.0, 1.0, ALU.mult, ALU.mult),
         reads=['vecs'], writes=['qgain'])
    def load_x_tile(tt):
        for c in range(NC_):
            P.op('sp', I('dma_start', out=xap(c, tt), in_=xT_d[c * 128:(c + 1) * 128, tt * TT:(tt + 1) * TT]),
                 writes=[xk(c, tt)], dma=True)

    def load_w(dst, dkey, src2d, KC, N, gain_col0=None):
        W = stage[0].shape[-1]
        for kc in range(KC):
            for n0 in range(0, N, W):
                n1 = min(N, n0 + W)
                r = next_stage()
                P.op('sp', I('dma_start', out=stage[r][:, 0:n1 - n0], in_=src2d[kc * 128:(kc + 1) * 128, n0:n1]),
                     writes=[f'stage{r}'], dma=True)
                if gain_col0 is None:
                    P.op('pool', I('tensor_copy', dst[:, kc, n0:n1], stage[r][:, 0:n1 - n0]),
                         reads=[f'stage{r}'], writes=[f'{dkey}.{kc}'])
                else:
                    P.op('pool', I('tensor_scalar', dst[:, kc, n0:n1], stage[r][:, 0:n1 - n0],
                                   vecs[:, gain_col0 + kc:gain_col0 + kc + 1], 1.0, ALU.mult, ALU.mult),
                         reads=[f'stage{r}', 'vecs'], writes=[f'{dkey}.{kc}'])

    def mm_group(bank_ap, pairs, bkey, reads):
        n = len(pairs)
        fs = [I('matmul', bank_ap, l, r, start=(i == 0), stop=(i == n - 1)) for i, (l, r) in enumerate(pairs)]
        P.op('pe', SEQ(*fs), reads=reads, writes=[bkey])

    def rms_rstd(tt, slot):
        b = next_bank()
        for c in range(NC_):
            j = c % 2
            P.op('act', I('activation', out=sqb[j][:], in_=xap(c, tt), func=AF.Square),
                 reads=[xk(c, tt)], writes=[f'sqb{j}'])
            P.op('pe', I('matmul', ps[:, b, :], ond, sqb[j][:], start=(c == 0), stop=(c == NC_ - 1)),
                 reads=[f'sqb{j}', 'cstb'], writes=[f'ps{b}'])
        P.op('act', I('activation', out=lnv[slot][:], in_=ps[:, b, :], func=AF.Ln, bias=EPS),
             reads=[f'ps{b}'], writes=[f'lnv{slot}'])
        P.op('act', I('activation', out=rstd[slot][:], in_=lnv[slot][:], func=AF.Exp, scale=-0.5),
             reads=[f'lnv{slot}'], writes=[f'lnv{slot}'])

    def ple_phase(layer, Wple, Wgate, B):
        seen = set()

        def wk(k):
            if k in seen:
                return [k]
            seen.add(k)
            return [k] + B.get('alias', {}).get(k, [])
        def bufs(tt):
            alt = (tt % 2 == 1) and ('x1b2' in B)
            return ((B['x1b2'], B['x1b2k'], B['pb2'], B['pb2k']) if alt else (B['x1b'], B['x1bk'], B['pb'], B['pbk']))

        def prep(tt):
            x1b, x1bk, pbs, pbks = bufs(tt)
            t0 = tt * TT
            for kc in range(2):
                P.op('pool', I('dma_start', out=pbs[kc], in_=pT_d[layer, kc * 128:(kc + 1) * 128, t0:t0 + TT]),
                     writes=wk(pbks[kc]), dma=True)
            for c in range(NC_):
                P.op('act', I('activation', out=x1b[c], in_=xap(c, tt), func=AF.Copy),
                     reads=[xk(c, tt)], writes=wk(x1bk[c]))

        prep(0)
        for tt in range(NTT):
            t0 = tt * TT
            x1b, x1bk, pbs, pbks = bufs(tt)
            if tt + 1 < NTT and 'x1b2' in B:
                prep(tt + 1)
            for n in range(NC_):
                ba = next_bank()
                bg = next_bank()
                mm_group(ps[:, ba, :], [(Wple[:, kc, n * 128:(n + 1) * 128], pbs[kc]) for kc in range(2)],
                         f'ps{ba}', B['wplek'] + pbks)
                mm_group(ps[:, bg, :], [(Wgate[:, kc, n * 128:(n + 1) * 128], x1b[kc]) for kc in range(NC_)],
                         f'ps{bg}', B['wgatek'] + x1bk)
                j = n % 2
                P.op('act', I('activation', out=B['sg'][j], in_=ps[:, bg, :], func=AF.Sigmoid),
                     reads=[f'ps{bg}'], writes=wk(B['sgk'][j]))
                P.op('dve', I('tensor_tensor', B['tm'][j], ps[:, ba, :], B['sg'][j], ALU.mult),
                     reads=[f'ps{ba}', B['sgk'][j]], writes=wk(B['tmk'][j]))
                P.op('pool', I('tensor_tensor', xap(n, tt), xap(n, tt), B['tm'][j], ALU.add),
                     reads=[B['tmk'][j], xk(n, tt)], writes=[xk(n, tt)])
                if B.get('store'):
                    P.op('sp', I('dma_start', out=out_d[n * 128:(n + 1) * 128, t0:t0 + TT], in_=xap(n, tt)),
                         reads=[xk(n, tt)], writes=[f'out.{n}.{tt}'], dma=True)
            if tt + 1 < NTT and 'x1b2' not in B:
                prep(tt + 1)

    def outproj_phase(Wout, y_ap_fn, ykeys_fn, tts):
        for tt in tts:
            for n in range(NC_):
                b = next_bank()
                mm_group(ps[:, b, :], [(Wout[:, dc, n * 128:(n + 1) * 128], y_ap_fn(dc, tt)) for dc in range(NC_)],
                         f'ps{b}', [f'wout.{k}' for k in range(NC_)] + [ykeys_fn(k, tt) for k in range(NC_)])
                P.op('dve', I('tensor_tensor', xap(n, tt), ps[:, b, :], xap(n, tt), ALU.add),
                     reads=[f'ps{b}', xk(n, tt)], writes=[xk(n, tt)])

    with ExitStack() as l0:
        def sb0(name, shape, dt):
            return l0.enter_context(nc.sbuf_tensor(name, list(shape), dt))
        Win = sb0("Win", [128, NC_, 2 * D], BF16)
        Wg = sb0("Wg", [128, 8, 256], BF16)
        Wout = sb0("Wout0", [128, NC_, D], BF16)
        xh = sb0("xh", [128, NC_, TT], BF16)
        ubuf = sb0("ubuf", [128, NC_, 16 + TT], F32)
        sAB = [sb0(f"sAB{i}", [128, 16 + TT], F32) for i in range(4)]
        pooled = [sb0(f"pooled{i}", [128, 2, TT], BF16) for i in range(2)]
        sz = [sb0(f"sz{i}", [128, 2, TT], BF16) for i in range(2)]
        y2 = [sb0(f"y0_{i}", [128, NC_, TT], BF16) for i in range(2)]
        fixt = sb0("fixt", [128, 16], F32)

        Wple0 = sb0("Wple0", [128, 2, D], BF16)
        Wgate0 = sb0("Wgate0", [128, NC_, D], BF16)

        def cast_dma(dst, dkeys, srcv):
            P.op('pool', I('dma_start', out=dst, in_=srcv), writes=dkeys, dma=True)

        def load_win_group(g):
            for h in range(2):
                c0 = h * D + g * 256
                srcv = awin_d.rearrange("(kc p) n -> p kc n", p=128)[:, :, c0:c0 + 256]
                cast_dma(Win[:, :, c0:c0 + 256], [f'win.g{g}'], srcv)

        P.op('pool', I('memset', ubuf[:], 0.0), writes=[f'ubuf{c}' for c in range(NC_)])
        load_x_tile(0)
        load_win_group(0)
        load_win_group(1)
        load_x_tile(1)
        load_win_group(2)
        load_win_group(3)
        cast_dma(Wg[:], [f'wg.{k}' for k in range(8)], awg_d.rearrange("(kc p) n -> p kc n", p=128))
        cast_dma(Wout[:], [f'wout.{k}' for k in range(NC_)], awout_d.rearrange("(kc p) n -> p kc n", p=128))
        load_x_tile(2)
        load_x_tile(3)
        cast_dma(Wple0[:], ['wple.0', 'wple.1'], plew_d[0].rearrange("(kc p) n -> p kc n", p=128))
        cast_dma(Wgate0[:], [f'wgate.{k}' for k in range(NC_)], gate_d[0].rearrange("(kc p) n -> p kc n", p=128))

        L = 16 + TT
        xh_keys = [f'xh.{k}' for k in range(NC_)]

        def xhmul(tt):
            slot = tt % 2
            for c in range(NC_):
                P.op('dve', I('scalar_tensor_tensor', xh[:, c, :], xap(c, tt), vecs[:, V_ANORM + c:V_ANORM + c + 1],
                              rstd[slot][:], ALU.mult, ALU.mult),
                     reads=[xk(c, tt), f'lnv{slot}', 'vecs'], writes=[f'xh.{c}'])

        def outproj_chunks(tt, ns):
            yb = tt % 2
            for n in ns:
                b = next_bank()
                mm_group(ps[:, b, :], [(Wout[:, dc, n * 128:(n + 1) * 128], y2[yb][:, dc, :]) for dc in range(NC_)],
                         f'ps{b}', [f'wout.{k}' for k in range(NC_)] + [f'y{yb}.{k}' for k in range(NC_)])
                P.op('dve', I('tensor_tensor', xap(n, tt), ps[:, b, :], xap(n, tt), ALU.add),
                     reads=[f'ps{b}', xk(n, tt)], writes=[xk(n, tt)])

        osched = {0: [2, 3], 1: [4, 5], 2: [6, 7], 3: []}
        rms_rstd(0, 0)
        xhmul(0)
        for tt in range(NTT):
            y = y2[tt % 2]
            ykey = f'y{tt % 2}'
            def front(g):
                w = POOL_W[g]
                pj = g % 2
                for cl in range(2):
                    c = 2 * g + cl
                    bu = next_bank()
                    mm_group(ps[:, bu, :], [(Win[:, kc, c * 128:(c + 1) * 128], xh[:, kc, :]) for kc in range(NC_)],
                             f'ps{bu}', xh_keys + [f'win.g{g}'])
                    P.op('act', I('activation', out=ubuf[:, c, 16:16 + TT], in_=ps[:, bu, :], func=AF.Copy),
                         reads=[f'ps{bu}'], writes=[f'ubuf{c}'])
                    bz = next_bank()
                    mm_group(ps[:, bz, :], [(Win[:, kc, D + c * 128:D + (c + 1) * 128], xh[:, kc, :]) for kc in range(NC_)],
                             f'ps{bz}', xh_keys + [f'win.g{g}'])
                    P.op('act', I('activation', out=sz[pj][:, cl, :], in_=ps[:, bz, :], func=AF.Silu),
                         reads=[f'ps{bz}'], writes=[f'sz{pj}.{cl}'])
                    src = ubuf[:, c, :]
                    skey = f'ubuf{c}'
                    bufs = [sAB[2 * cl], sAB[2 * cl + 1]]
                    bkeys = [f'sAB{2 * cl}', f'sAB{2 * cl + 1}']
                    step, lo, bi = 1, 0, 0
                    while step < w:
                        lo2 = lo + step
                        dst = bufs[bi]
                        P.op('dve', I('tensor_tensor', dst[:, lo2:L], src[:, lo2:L], src[:, lo2 - step:L - step], ALU.add),
                             reads=[skey], writes=[bkeys[bi]])
                        src = dst[:, :]
                        skey = bkeys[bi]
                        bi ^= 1
                        lo = lo2
                        step *= 2
                    P.op('dve', I('scalar_tensor_tensor', pooled[pj][:, cl, :], src[:, 16:16 + TT], 1.0 / w,
                                  ubuf[:, c, 16:16 + TT], ALU.mult, ALU.subtract),
                         reads=[skey, f'ubuf{c}'], writes=[f'pooled{pj}.{cl}'])
                    if tt == 0:
                        P.op('dve', I('tensor_tensor', fixt[:, 0:w - 1], src[:, 16:16 + w - 1],
                                      rc[:, g * 16:g * 16 + w - 1], ALU.mult),
                             reads=[skey, 'cstf'], writes=['fixt'])
                        P.op('dve', I('tensor_tensor', pooled[pj][:, cl, 0:w - 1], fixt[:, 0:w - 1],
                                      ubuf[:, c, 16:16 + w - 1], ALU.subtract),
                             reads=['fixt', f'ubuf{c}'], writes=[f'pooled{pj}.{cl}'])
                    P.op('pool', I('tensor_copy', ubuf[:, c, 0:16], ubuf[:, c, TT:TT + 16]),
                         reads=[f'ubuf{c}'], writes=[f'ubuf{c}'])

            def back(g):
                pj = g % 2
                if tt > 0:
                    outproj_chunks(tt - 1, osched[g])
                for dl in range(2):
                    dc = 2 * g + dl
                    bm = next_bank()
                    mm_group(ps[:, bm, :], [(Wg[:, 2 * g + kc, dl * 128:(dl + 1) * 128], pooled[pj][:, kc, :]) for kc in range(2)],
                             f'ps{bm}', [f'wg.{2 * g}', f'wg.{2 * g + 1}', f'pooled{pj}.0', f'pooled{pj}.1'])
                    P.op('dve', I('scalar_tensor_tensor', y[:, dc, :], ps[:, bm, :],
                                  vecs[:, V_ASCALE + dc:V_ASCALE + dc + 1], sz[pj][:, dl, :], ALU.mult, ALU.mult),
                         reads=[f'ps{bm}', 'vecs', f'sz{pj}.{dl}'], writes=[f'{ykey}.{dc}'])
                if g == 1 and tt + 1 < NTT:
                    rms_rstd(tt + 1, (tt + 1) % 2)

            front(0)
            for g in range(4):
                if g + 1 < 4:
                    front(g + 1)
                back(g)
            outproj_chunks(tt, [0, 1])
            if tt + 1 < NTT:
                xhmul(tt + 1)
        outproj_chunks(NTT - 1, [2, 3, 4, 5, 6, 7])
        B0 = dict(x1b=[xh[:, c, :] for c in range(NC_)], x1bk=[f'xh.{c}' for c in range(NC_)],
                  pst=[ubuf[:, k, 16:16 + TT] for k in range(2)], pstk=['ubuf0', 'ubuf1'],
                  pb=[pooled[0][:, k, :] for k in range(2)], pbk=['pooled0.0', 'pooled0.1'],
                  x1b2=[y2[0][:, c, :] for c in range(NC_)], x1b2k=[f'y0.{c}' for c in range(NC_)],
                  pb2=[pooled[1][:, k, :] for k in range(2)], pb2k=['pooled1.0', 'pooled1.1'],
                  sg=[sAB[0][:, 0:TT], sAB[1][:, 0:TT]], sgk=['sAB0', 'sAB1'],
                  tm=[sAB[2][:, 0:TT], sAB[3][:, 0:TT]], tmk=['sAB2', 'sAB3'],
                  wplek=['wple.0', 'wple.1'], wgatek=[f'wgate.{k}' for k in range(NC_)])
        ple_phase(0, Wple0, Wgate0, B0)
        P.emit()

    def ple_scope(layer):
        with ExitStack() as l:
            def sbl(name, shape, dt):
                return l.enter_context(nc.sbuf_tensor(name, list(shape), dt))
            stage[:] = [sbl(f"stagep{layer}_{i}", [128, D], F32) for i in range(2)]
            Wple = sbl(f"Wple{layer}", [128, 2, D], BF16)
            Wgate = sbl(f"Wgate{layer}", [128, NC_, D], BF16)
            x1b = sbl(f"x1b{layer}", [128, NC_, TT], BF16)
            pstage = sbl(f"pstage{layer}", [128, 2, TT], F32)
            pb = sbl(f"pb{layer}", [128, 2, TT], BF16)
            sg = [sbl(f"sg{layer}{i}", [128, TT], F32) for i in range(2)]
            tm = [sbl(f"tm{layer}{i}", [128, TT], F32) for i in range(2)]
            load_w(Wple, 'wple', plew_d[layer], 2, D)
            load_w(Wgate, 'wgate', gate_d[layer], NC_, D)
            B = dict(x1b=[x1b[:, c, :] for c in range(NC_)], x1bk=[f'x1b.{c}' for c in range(NC_)],
                     pst=[pstage[:, k, :] for k in range(2)], pstk=['pstage.0', 'pstage.1'],
                     pb=[pb[:, k, :] for k in range(2)], pbk=['pb.0', 'pb.1'],
                     sg=[sg[0][:], sg[1][:]], sgk=['sg0', 'sg1'], tm=[tm[0][:], tm[1][:]], tmk=['tm0', 'tm1'],
                     wplek=['wple.0', 'wple.1'], wgatek=[f'wgate.{k}' for k in range(NC_)])
            ple_phase(layer, Wple, Wgate, B)
            P.emit()


    if n_layers >= 2:
        with ExitStack() as l1:
            y1 = l1.enter_context(nc.sbuf_tensor("y1", [128, NC_, S], BF16))
            with ExitStack() as l1a:
                def sba(name, shape, dt):
                    return l1a.enter_context(nc.sbuf_tensor(name, list(shape), dt))
                xh1 = sba("xh1", [128, NC_, S], BF16)
                KT2 = [sba(f"KT{i}", [128, S], BF16) for i in range(2)]
                QT2 = [sba(f"QT{i}", [128, S], BF16) for i in range(2)]
                Vt2 = [sba(f"Vt{i}", [128, 16, 128], BF16) for i in range(2)]
                szc2 = [sba("szc0", [128, S], BF16)] * 2
                wsl = {k: sba(f"wsl_{k}", [128, NC_, 128], BF16) for k in ('k', 'v', 'q', 'z')}
                stg1 = [sba("stg1_0", [128, NC_, 128], F32)] * 2
                zeros128 = sba("zeros128", [128, 128], BF16)
                eg = [sba(f"eg{i}", [128, 4, TT], F32) for i in range(3)]

                def e_ap(n):
                    s = n % 3
                    return eg[s][:, 0:2, :], f'eg{s}.e'

                def g_ap(n):
                    s = (n + 2) % 3
                    return eg[s][:, 2:4, :], f'eg{s}.g'
                spb = [sba(f"spb{i}", [128, 2, TT], BF16) for i in range(2)]
                abuf = [sba("abuf0", [128, 2, TT], BF16)] * 2
                rrow2 = sba("rrow2", [64, TT], BF16)
                LB, OB = 0, 4
                spare = [5, 6]
                sp_i = [0]

                owned = set()

                def spare_bank():
                    b = spare[sp_i[0] % len(spare)]
                    sp_i[0] += 1
                    assert b not in owned, f"spare bank {b} still owned"
                    owned.add(b)
                    return b

                def release(b):
                    owned.discard(b)

                P.op('pool', I('memset', zeros128[:], 0.0), writes=['zeros128'])

                st1_i = [0]

                def proj_items(cc, which):
                    bs = cc % 2
                    KT, QT, Vt, szc = KT2[bs], QT2[bs], Vt2[bs], szc2[bs]
                    items = []

                    def w_item(nm, src, col0):
                        def st0():
                            P.op('sp', I('dma_start', out=stg1[0][:], in_=src.rearrange("(kc p) n -> p kc n", p=128)),
                                 writes=['stg1'], dma=True)

                        def cast(kcs):
                            def f():
                                for kc in kcs:
                                    P.op('pool', I('tensor_scalar', wsl[nm][:, kc, :], stg1[0][:, kc, :],
                                                   vecs[:, col0 + kc:col0 + kc + 1], 1.0, ALU.mult, ALU.mult),
                                         reads=['stg1', 'vecs'], writes=[f'wsl{nm}'])
                            return f
                        if 'W' in which:
                            return [st0, lambda: None, cast([0, 1, 2, 3]), cast([4, 5, 6, 7])]
                        return [lambda: (st0(), cast(range(NC_))())]

                    def kq_item(nm, dst, dkey, gain_ap, tt):
                        st = {}

                        def st0():
                            st['b'] = spare_bank()
                            b = st['b']
                            mm_group(ps[:, b, :], [(wsl[nm][:, kc, :], xh1[:, kc, tt * TT:(tt + 1) * TT]) for kc in range(NC_)],
                                     f'ps{b}', [f'wsl{nm}'] + [f'xh1.{k}.{tt}' for k in range(NC_)])

                        def st1():
                            b = st['b']
                            j = tt % 2
                            st['j'] = j
                            P.op('act', I('activation', out=sqb[j][:], in_=ps[:, b, :], func=AF.Square),
                                 reads=[f'ps{b}'], writes=[f'sqb{j}'])
                            st['b2'] = spare_bank()
                            b2 = st['b2']
                            P.op('pe', I('matmul', ps[:, b2, :], blk, sqb[j][:], start=True, stop=True),
                                 reads=[f'sqb{j}', 'cstb'], writes=[f'ps{b2}'])

                        def st2():
                            b, b2, j = st['b'], st['b2'], st['j']
                            P.op('act', I('activation', out=lnv[j][:], in_=ps[:, b2, :], func=AF.Ln, bias=EPS),
                                 reads=[f'ps{b2}'], writes=[f'lnv{j}'])
                            P.op('act', I('activation', out=rstd[j][:], in_=lnv[j][:], func=AF.Exp, scale=-0.5),
                                 reads=[f'lnv{j}'], writes=[f'lnv{j}'])
                            P.op('dve', I('scalar_tensor_tensor', dst[:, tt * TT:(tt + 1) * TT], ps[:, b, :], gain_ap,
                                          rstd[j][:], ALU.mult, ALU.mult),
                                 reads=[f'ps{b}', f'lnv{j}', 'vecs', 'qgain'], writes=[f'{dkey}{bs}.{tt}'])
                            release(b)
                            release(b2)
                        return [st0, st1, st2]

                    def v_item(bk):
                        st = {}

                        def st0():
                            st['b'] = spare_bank()
                            b = st['b']
                            mm_group(ps[:, b, 0:128], [(xh1[:, kc, bk * 128:(bk + 1) * 128], wsl['v'][:, kc, :]) for kc in range(NC_)],
                                     f'ps{b}', ['wslv'] + [f'xh1.{k}.{bk // 4}' for k in range(NC_)])

                        def st1():
                            b = st['b']
                            P.op('dve', I('tensor_copy', Vt[:, bk, :], ps[:, b, 0:128]),
                                 reads=[f'ps{b}'], writes=[f'Vt{bs}.{bk}'])
                            release(b)
                        return [st0, st1]

                    def z_item(tt):
                        st = {}

                        def st0():
                            st['b'] = spare_bank()
                            b = st['b']
                            mm_group(ps[:, b, :], [(wsl['z'][:, kc, :], xh1[:, kc, tt * TT:(tt + 1) * TT]) for kc in range(NC_)],
                                     f'ps{b}', ['wslz'] + [f'xh1.{k}.{tt}' for k in range(NC_)])

                        def st1():
                            b = st['b']
                            P.op('act', I('activation', out=lnv[0][:], in_=ps[:, b, :], func=AF.Exp, scale=-1.0),
                                 reads=[f'ps{b}'], writes=['lnv0'])
                            P.op('act', I('activation', out=lnv[0][:], in_=lnv[0][:], func=AF.Ln, bias=1.0),
                                 reads=['lnv0'], writes=['lnv0'])
                            P.op('act', I('activation', out=lnv[0][:], in_=lnv[0][:], func=AF.Exp, scale=-1.0),
                                 reads=['lnv0'], writes=['lnv0'])
                            P.op('dve', I('tensor_tensor', szc[:, tt * TT:(tt + 1) * TT], ps[:, b, :], lnv[0][:], ALU.mult),
                                 reads=['lnv0', f'ps{b}'], writes=[f'szc.{tt}'])
                            release(b)
                        return [st0, st1]

                    if 'w' in which or 'W' in which:
                        for wi in (w_item('k', wkv_d[cc], V_KVN), w_item('q', bwin_d[cc], V_BN),
                                   w_item('v', wkv_d[8 + cc], V_KVN), w_item('z', bwin_d[8 + cc], V_BN)):
                            items.append(wi)
                            if 'W' in which:
                                items.extend([[] for _ in range(3)])
                    kq = []
                    if 'k' in which or 'K' in which:
                        for tt in range(NTT):
                            kq.append(kq_item('k', KT, 'KT', nkgain[:], tt))
                    if 'k' in which or 'Q' in which:
                        for tt in range(NTT):
                            kq.append(kq_item('q', QT, 'QT', qgain[:], tt))
                    if 'K' in which or 'Q' in which:
                        return kq
                    vz = []
                    if 'v' in which:
                        for bk in range(16):
                            vz.append(v_item(bk))
                    if 'z' in which:
                        for tt in range(NTT):
                            vz.append(z_item(tt))
                    if which in ('v', 'z'):
                        return vz
                    while kq or vz:
                        if kq:
                            items.append(kq.pop(0))
                        for _ in range(2):
                            if vz:
                                items.append(vz.pop(0))
                    return items

                def run_stages(items, i, order=(5, 4, 3, 2, 1, 0)):
                    for s in order:
                        k = i - s
                        if 0 <= k < len(items) and s < len(items[k]):
                            items[k][s]()

                spare[:] = [5, 6, 7, 0, 1, 2, 3]
                itw = proj_items(0, 'w')
                for i in range(len(itw) + 6):
                    run_stages(itw, i)
                for tt in range(NTT):
                    slot = tt % 2
                    rms_rstd(tt, slot)
                    for c in range(NC_):
                        P.op('dve', I('tensor_tensor', xh1[:, c, tt * TT:(tt + 1) * TT], xap(c, tt), rstd[slot][:], ALU.mult),
                             reads=[xk(c, tt), f'lnv{slot}'], writes=[f'xh1.{c}.{tt}'])
                it0 = proj_items(0, 'vkz')
                for i in range(len(it0) + 6):
                    run_stages(it0, i, order=(0, 1, 2, 3, 4, 5))

                Wout1v = xh1[:, 0:4, :].rearrange("p a (b n) -> p (a b) n", b=2)
                Wgate1v = xh1[:, 4:8, :].rearrange("p a (b n) -> p (a b) n", b=2)
                Wple1v = KT2[0][:, :].rearrange("p (a n) -> p a n", a=2)
                stgA = stg1[0][:].rearrange("p kc n -> p (kc n)")
                stgB = QT2[0].bitcast(F32)
                xh1_lo = [f'xh1.{cc}.{t}' for cc in range(0, 4) for t in range(NTT)]
                xh1_hi = [f'xh1.{cc}.{t}' for cc in range(4, 8) for t in range(NTT)]
                kt0_keys = [f'KT0.{t}' for t in range(NTT)]
                qt0_keys = [f'QT0.{t}' for t in range(NTT)]

                def end_items():
                    def st0():
                        P.op('pool', I('dma_start', out=Wout1v, in_=bwout_d.rearrange("(kc p) n -> p kc n", p=128)),
                             writes=[f'wout.{k}' for k in range(NC_)] + xh1_lo, dma=True)
                        P.op('pool', I('dma_start', out=Wgate1v, in_=gate_d[1].rearrange("(kc p) n -> p kc n", p=128)),
                             writes=[f'wgate.{k}' for k in range(NC_)] + xh1_hi, dma=True)
                        P.op('pool', I('dma_start', out=Wple1v, in_=plew_d[1].rearrange("(kc p) n -> p kc n", p=128)),
                             writes=['wple.0', 'wple.1'] + kt0_keys, dma=True)
                    return [[], [], [st0]]

                spare[:] = [7]
                sp_i[0] = 0
                def outproj_item(n, tt, dcs):
                    st = {}

                    def st0():
                        st['b'] = spare_bank()
                        b = st['b']
                        mm_group(ps[:, b, :], [(Wout1v[:, dc, n * 128:(n + 1) * 128], y1[:, dc, tt * TT:(tt + 1) * TT]) for dc in dcs],
                                 f'ps{b}', [f'wout.{k}' for k in dcs] + [f'y1.{k}.{tt}' for k in dcs])

                    def st1():
                        b = st['b']
                        P.op('dve', I('tensor_tensor', xap(n, tt), ps[:, b, :], xap(n, tt), ALU.add),
                             reads=[f'ps{b}', xk(n, tt)], writes=[xk(n, tt)])
                        release(b)
                    return [st0, st1]

                tile_n = 0
                for c in range(DBG['nchunks']):
                    bs = c % 2
                    KT, QT, Vt, szc = KT2[bs], QT2[bs], Vt2[bs], szc2[bs]
                    if c + 1 < DBG['nchunks']:
                        nxt = proj_items(c + 1, 'W')
                        kk = proj_items(c + 1, 'K')
                        qq = proj_items(c + 1, 'Q')
                        vv = proj_items(c + 1, 'v')
                        zz = proj_items(c + 1, 'z')
                        nxtP = [[] for _ in range(41)]
                        for t_, it in enumerate(vv):
                            nxtP[14 + t_] = it
                        for t_, it in enumerate(zz[:3]):
                            nxtP[30 + t_] = it
                        nxtP[40] = zz[3]
                        for t_, it in enumerate(kk):
                            nxtP[4 + t_] = it
                        for t_, it in enumerate(qq):
                            nxtP[8 + t_] = it
                        kq_next = []
                    else:
                        nxt = end_items()
                        kq_next = []
                        nxtP = [[] for _ in range(10)]
                        for tt_ in range(NTT):
                            for n_ in range(NC_):
                                nxtP.append(outproj_item(n_, tt_, list(range(NC_ - 1))))
                    seq = []
                    for qt in range(NTT):
                        for kb in range(4 * qt + 3, -1, -1):
                            seq.append((qt, kb))
                    if DBG['ntiles'] is not None:
                        seq = seq[:DBG['ntiles']]
                    NS = len(seq)

                    KQW = 14

                    def cb_of(i):
                        if i < KQW:
                            return 2
                        return 5 if (i - KQW) % 2 == 0 else 2

                    def tinfo(i):
                        qt, kb = seq[i]
                        j = kb - 4 * qt
                        lo = 128 * j if j > 0 else 0
                        first = (kb == 4 * qt + 3)
                        if first:
                            lor = None
                        else:
                            jn = j + 1
                            lor = 128 * jn + 1 if jn >= 0 else 0
                        return qt, kb, j, lo, first, lor, tile_n + i

                    def rec_qk(i):
                        qt, kb, j, lo, first, lor, n = tinfo(i)
                        t0 = qt * TT
                        fs = []
                        for h in range(2):
                            p0 = 64 * h
                            fs.append(I('matmul', ps[:, LB + h, lo:TT], KT[p0:p0 + 64, kb * 128:(kb + 1) * 128],
                                        QT[p0:p0 + 64, t0 + lo:t0 + TT], start=True, stop=(j < 0)))
                        if j >= 0:
                            for h in range(2):
                                fs.append(I('matmul', ps[:, LB + h, lo:lo + 128], ident, negp, start=False, stop=True))
                        P.op('pe', SEQ(*fs), reads=[f'KT{bs}.{kb // 4}', f'QT{bs}.{qt}', 'cstb'],
                             writes=[f'ps{LB}', f'ps{LB + 1}'])

                    def rec_p1(i):
                        qt, kb, j, lo, first, lor, n = tinfo(i)
                        ea, ek = e_ap(n)
                        P.op('act', I('activation', out=ea[:, :, lo:TT], in_=ps[:, LB:LB + 2, lo:TT], func=AF.Exp, scale=-1.0),
                             reads=[f'ps{LB}', f'ps{LB + 1}'], writes=[ek])

                    def rec_p2(i):
                        qt, kb, j, lo, first, lor, n = tinfo(i)
                        ea, ek = e_ap(n)
                        sj = n % 2
                        P.op('act', I('activation', out=spb[sj][:, :, lo:TT], in_=ea[:, :, lo:TT], func=AF.Ln, bias=1.0),
                             reads=[ek], writes=[f'spb{sj}'])

                    def rec_p3p1(i):
                        has3 = 1 <= i <= NS
                        has1 = i + 1 < NS
                        merged = False
                        if has3 and has1:
                            t3, t1 = tinfo(i - 1), tinfo(i + 1)
                            if t3[3] == 0 and t1[3] == 0 and cb_of(i - 1) == LB + 2:
                                s = t1[6] % 3
                                assert (t3[6] + 2) % 3 == s
                                P.op('act', I('activation', out=eg[s][:, :, :], in_=ps[:, LB:LB + 4, :], func=AF.Exp, scale=-1.0),
                                     reads=[f'ps{LB}', f'ps{LB + 1}', f'ps{LB + 2}', f'ps{LB + 3}'],
                                     writes=[f'eg{s}.e', f'eg{s}.g'])
                                merged = True
                        if has3:
                            rec_p3(i - 1, act=not merged)
                        if has1 and not merged:
                            rec_p1(i + 1)

                    def rec_tri(i):
                        qt, kb, j, lo, first, lor, n = tinfo(i)
                        CB = cb_of(i)
                        sj = n % 2
                        rj = n % 2
                        fs = []
                        for h in range(2):
                            fs.append(I('matmul', ps[:, CB + h, lo:TT], tri, spb[sj][:, h, lo:TT], start=True, stop=(lor is None)))
                        P.op('pe', SEQ(*fs), reads=[f'spb{sj}', 'cstb'], writes=[f'ps{CB}', f'ps{CB + 1}'])
                        if lor is not None:
                            fs = []
                            for h in range(2):
                                fs.append(I('matmul', ps[:, CB + h, lor:TT], onef[32 * h:32 * h + 1, :],
                                            rrow2[32 * h:32 * h + 1, lor:TT], start=False, stop=True))
                            P.op('pe', SEQ(*fs), reads=['rrow', 'cstb'], writes=[f'ps{CB}', f'ps{CB + 1}'])

                    def rec_rcopy(i):
                        qt, kb, j, lo, first, lor, n = tinfo(i)
                        CB = cb_of(i)
                        if kb > 0:
                            lorn = 128 * j + 1 if j >= 0 else 0
                            for h in range(2):
                                P.op('dve', I('tensor_copy', rrow2[32 * h:32 * h + 1, lorn:TT], ps[0:1, CB + h, lorn:TT]),
                                     reads=[f'ps{CB + h}'], writes=['rrow'])

                    def rec_p3(i, act=True):
                        qt, kb, j, lo, first, lor, n = tinfo(i)
                        CB = cb_of(i)
                        ea, ek = e_ap(n)
                        ga, gk = g_ap(n)
                        if act:
                            P.op('act', I('activation', out=ga[:, :, lo:TT], in_=ps[:, CB:CB + 2, lo:TT], func=AF.Exp, scale=-1.0),
                                 reads=[f'ps{CB}', f'ps{CB + 1}'], writes=[gk])
                        rec_rcopy(i)
                        P.op('dve', I('tensor_tensor', abuf[0][:, 0, lo:TT], ea[:, 0, lo:TT], ga[:, 0, lo:TT], ALU.mult),
                             reads=[ek, gk], writes=['abuf.0'])
                        P.op('pool', I('tensor_tensor', abuf[0][:, 1, lo:TT], ea[:, 1, lo:TT], ga[:, 1, lo:TT], ALU.mult),
                             reads=[ek, gk], writes=['abuf.1'])

                    def rec_av(i):
                        qt, kb, j, lo, first, lor, n = tinfo(i)
                        aj = n % 2
                        fs = []
                        if first:
                            fs.append(I('matmul', ps[:, OB, :], zeros128[:], KT[:, 0:TT], start=True, stop=False))
                        for h in range(2):
                            p0 = 64 * h
                            fs.append(I('matmul', ps[p0:p0 + 64, OB, lo:TT], Vt[:, kb, p0:p0 + 64], abuf[aj][:, h, lo:TT],
                                        start=False, stop=(kb == 0)))
                        P.op('pe', SEQ(*fs), reads=['abuf.0', 'abuf.1', f'Vt{bs}.{kb}', 'zeros128', f'KT{bs}.0'],
                             writes=[f'ps{OB}'])
                        if kb == 0:
                            t0 = qt * TT
                            P.op('dve', I('tensor_tensor', y1[:, c, t0:t0 + TT], ps[:, OB, :], szc[:, t0:t0 + TT], ALU.mult),
                                 reads=[f'ps{OB}', f'szc.{qt}'], writes=[f'y1.{c}.{qt}'])

                    if NS > 0:
                        rec_qk(0)
                        rec_p1(0)
                    nsteps = max(NS + 1 if NS > 0 else 0, len(nxt) + 6 if nxt else 0, len(nxtP) + 6 if nxtP else 0)
                    for i in range(nsteps):
                        if i + 1 < NS:
                            rec_qk(i + 1)
                        spare[:] = [7, 5, 6] if i < KQW else [7]
                        run_stages(nxtP, i)
                        if i < NS:
                            light = i < len(nxtP) and len(nxtP[i]) == 2
                            heavy = i < len(nxtP) and len(nxtP[i]) == 3
                            nd = 0 if heavy else (2 if light else 4)
                            o_open = i >= 2 and not tinfo(i - 1)[4] and not tinfo(i - 2)[1] == 0
                            if nd and o_open and DBG.get('dummy', 1):
                                fs = [I('matmul', ps[:, OB, :], zeros128[:], KT[:, 0:TT], start=False, stop=False) for _ in range(nd)]
                                P.op('pe', SEQ(*fs), reads=[f'KT{bs}.0', 'zeros128'], writes=[f'ps{OB}'])
                        if i < NS:
                            rec_p2(i)
                        rec_p3p1(i)
                        if i < NS:
                            rec_tri(i)
                        run_stages(nxt, i)
                        if 1 <= i <= NS:
                            rec_av(i - 1)
                    tile_n += NS
                    if kq_next:
                        assert not owned, owned
                        spare[:] = [7, 0, 1, 5, 6, 2, 3]
                        for ii in range(len(kq_next) + 6):
                            run_stages(kq_next, ii, order=(0, 1, 2, 3, 4, 5))
                        assert not owned, owned
                        spare[:] = [7]

                last = NC_ - 1 if DBG['nchunks'] == NC_ else None
                for tt in range(NTT):
                    for n in range(NC_):
                        b = next_bank()
                        dcs = [last] if last is not None else list(range(NC_))
                        mm_group(ps[:, b, :], [(Wout1v[:, dc, n * 128:(n + 1) * 128], y1[:, dc, tt * TT:(tt + 1) * TT]) for dc in dcs],
                                 f'ps{b}', [f'wout.{k}' for k in dcs] + [f'y1.{k}.{tt}' for k in dcs])
                        P.op('dve', I('tensor_tensor', xap(n, tt), ps[:, b, :], xap(n, tt), ALU.add),
                             reads=[f'ps{b}', xk(n, tt)], writes=[xk(n, tt)])
                B1 = dict(x1b=[szc2[0][:, cc * TT:(cc + 1) * TT] for cc in range(4)] + [spb[i][:, h, :] for i in range(2) for h in range(2)],
                          x1bk=[f'szc.{cc}' for cc in range(4)] + ['x1b.4', 'x1b.5', 'x1b.6', 'x1b.7'],
                          pst=[eg[0][:, k, :] for k in range(2)], pstk=['pst.0', 'pst.1'],
                          alias={'x1b.4': ['spb0'], 'x1b.5': ['spb0'], 'x1b.6': ['spb1'], 'x1b.7': ['spb1'],
                                 'pst.0': ['eg0.e'], 'pst.1': ['eg0.e'], 'pb2.0': ['eg0.g'], 'pb2.1': ['eg0.g'],
                                 'tm.0': ['eg0.g'], 'tm.1': ['eg1.g']},
                          pb=[abuf[0][:, k, :] for k in range(2)], pbk=['abuf.0', 'abuf.1'],
                          x1b2=[KT2[1][:, cc * TT:(cc + 1) * TT] for cc in range(4)] + [QT2[1][:, cc * TT:(cc + 1) * TT] for cc in range(4)],
                          x1b2k=[f'KT1.{cc}' for cc in range(4)] + [f'QT1.{cc}' for cc in range(4)],
                          pb2=[eg[0].bitcast(BF16)[:, 3, k * TT:(k + 1) * TT] for k in range(2)],
                          pb2k=['pb2.0', 'pb2.1'],
                          sg=[lnv[0][:], lnv[1][:]], sgk=['lnv0', 'lnv1'],
                          tm=[eg[0][:, 2, :], eg[1][:, 2, :]], tmk=['tm.0', 'tm.1'],
                          wplek=['wple.0', 'wple.1'], wgatek=[f'wgate.{k}' for k in range(NC_)], store=True)
                ple_phase(1, Wple1v, Wgate1v, B1)
                P.emit()

    if n_layers < 2:
        for c in range(NC_):
            P.op('sp', I('dma_start', out=out_d[c * 128:(c + 1) * 128, :], in_=xT[:, c, :]),
                 reads=[xk(c, t) for t in range(NTT)], dma=True)
        P.emit()
    P.final_wait('sp')
    es.close()
    return P


def make_consts():
    c = np.zeros((128, NCB * 128), np.float32)
    idx = np.arange(128)
    c[:, C_ID * 128:(C_ID + 1) * 128] = np.eye(128, dtype=np.float32)
    c[:, C_TRI * 128:(C_TRI + 1) * 128] = (idx[:, None] >= idx[None, :]).astype(np.float32)
    c[:, C_BLK * 128:(C_BLK + 1) * 128] = ((idx[:, None] // 64) == (idx[None, :] // 64)).astype(np.float32) / 64.0
    c[:, C_OND * 128:(C_OND + 1) * 128] = 1.0 / D
    neg = (idx[:, None] >= idx[None, :]).astype(np.float32) * BIG
    c[:, C_NEGP * 128:(C_NEGP + 1) * 128] = neg
    c[:, C_NEGN * 128:(C_NEGN + 1) * 128] = -neg
    c[:, C_ONE * 128:(C_ONE + 1) * 128] = 1.0
    rc = np.zeros((128, 64), np.float32)
    for g, w in enumerate(POOL_W):
        t = np.arange(16)
        rc[:, g * 16:(g + 1) * 16] = 1.0 / np.minimum(t + 1, w).astype(np.float32)
    import ml_dtypes
    return c.astype(ml_dtypes.bfloat16), rc


_CONSTS = make_consts()


def prep_inputs(x, p, a_norm, a_w_in, a_w_group, a_scale, a_w_out, kv_norm, w_kv, k_norm,
                b_norm, b_w_in, b_q_norm, b_w_out, ple_w, ple_gate_w):
    f = lambda a: np.ascontiguousarray(np.asarray(a, dtype=np.float32))
    x = f(x); p = f(p)
    B = x.shape[0]
    vecs = np.zeros((128, NVEC), np.float32)
    col = lambda v: f(v).reshape(NC_, 128).T
    vecs[:, V_ANORM:V_ANORM + 8] = col(a_norm[0])
    vecs[:, V_ASCALE:V_ASCALE + 8] = col(a_scale[0])
    vecs[:, V_KVN:V_KVN + 8] = col(kv_norm)
    vecs[:, V_BN:V_BN + 8] = col(b_norm[0])
    vecs[:, V_KN] = np.tile(f(k_norm), 2)
    vecs[:, V_QN] = np.tile(f(b_q_norm[0]), 2)
    shared = {
        "vecs": vecs,
        "cst": _CONSTS[0],
        "rc": _CONSTS[1],
        "a_w_in": f(a_w_in[0]),
        "a_w_group": f(a_w_group[0]).reshape(D, 256),
        "a_w_out": f(a_w_out[0]),
        "w_kv_s": f(f(w_kv).reshape(D, 16, 128).transpose(1, 0, 2)),
        "b_w_in_s": f(f(b_w_in[0]).reshape(D, 16, 128).transpose(1, 0, 2)),
        "b_w_out": f(b_w_out[0]),
        "ple_w": f(ple_w),
        "ple_gate_w": f(ple_gate_w),
    }
    in_maps = []
    for b in range(B):
        m = dict(shared)
        m["xT"] = f(x[b].T)
        m["pT"] = f(p[:, b].transpose(0, 2, 1))
        in_maps.append(m)
    return in_maps


_CACHE = {}


def get_nc(n_layers=2):
    if n_layers not in _CACHE:
        nc = bass.Bass("TRN2", target_bir_lowering=False)
        build_program(nc, n_layers=n_layers)
        _CACHE[n_layers] = nc
    return _CACHE[n_layers]


def kernel(**inputs):
    in_maps = prep_inputs(**inputs)
    nc = get_nc(2)
    res = run_bass_kernel_spmd(nc, in_maps, core_ids=list(range(8)))
    out = np.stack([np.asarray(r["outT"], dtype=np.float32).T for r in res.results], axis=0)
    return np.ascontiguousarray(out)
```

```python
import numpy as np
import concourse.bass as bass
import concourse.mybir as mybir
from concourse.bass_utils import run_bass_kernel_spmd

F32 = mybir.dt.float32
BF16 = mybir.dt.bfloat16
AF = mybir.ActivationFunctionType
ALU = mybir.AluOpType

D = 1024
S = 2048
NC_ = 8
TT = 512
NTT = S // TT
PLE = 256
EPS = 1e-6
BIG = 30000.0
SB_SCALE = 64 ** -0.5
POOL_W = (2, 4, 8, 16)
DBG = {'nchunks': 8, 'ntiles': None, 'stage': 9, 'dummy': 1}

V_ANORM, V_ASCALE, V_KVN, V_BN, V_KN, V_QN = 0, 8, 16, 24, 32, 33
NVEC = 34
C_ID, C_TRI, C_BLK, C_OND, C_NEGP, C_NEGN, C_ONE = range(7)
NCB = 7


class Prog:
    CHUNK = 4000
    NDMA = 20

    def __init__(self, nc, sems):
        self.nc = nc
        self.engs = {'pe': nc.tensor, 'act': nc.scalar, 'dve': nc.vector,
                     'pool': nc.gpsimd, 'sp': nc.sync}
        self.free_sems = list(sems)
        self.esem = {e: [] for e in self.engs}
        self.ecount = {e: 0 for e in self.engs}
        self.dsem = [self.free_sems.pop() for _ in range(self.NDMA)]
        self.dcount = [0] * self.NDMA
        self.dlast = [None] * self.NDMA
        self.dnext = 0
        self.known = {e: {} for e in self.engs}
        self.last_w = {}
        self.readers = {}
        self.uid = 0
        self.ops = []
        self.events = {}
        self.phase_start = 0
        self.barrier_events = []
        self.n_inst = 0

    def op(self, eng, fn, reads=(), writes=(), dma=False):
        uid = self.uid
        self.uid += 1
        writes = tuple(writes) + tuple(k for k in reads if k.startswith('ps') and k not in writes)
        reads = tuple(k for k in reads if not k.startswith('ps'))
        deps = set()
        for k in reads:
            w = self.last_w.get(k)
            if w is not None:
                deps.add(w)
        for k in writes:
            w = self.last_w.get(k)
            if w is not None:
                deps.add(w)
            for r in self.readers.get(k, ()):
                deps.add(r)
        for k in reads:
            self.readers.setdefault(k, []).append(uid)
        for k in writes:
            self.last_w[k] = uid
            self.readers[k] = []
        deps.discard(uid)
        self.ops.append(dict(uid=uid, eng=eng, fn=fn, dma=dma, deps=deps))
        return uid

    def _wait(self, eng, ev):
        sem, val = ev
        kn = self.known[eng]
        if kn.get(id(sem), 0) >= val:
            return
        self.engs[eng].wait_ge(sem, val)
        kn[id(sem)] = val
        self.n_inst += 1

    def emit(self):
        ops = self.ops
        self.ops = []
        info = {o['uid']: o for o in ops}
        ps = self.phase_start
        signaling = set()
        for o in ops:
            best = {}
            dmas = []
            for d in o['deps']:
                if d < ps:
                    continue
                p = info[d]
                if p['dma']:
                    dmas.append(d)
                    continue
                if p['eng'] == 'pe' and o['eng'] == 'pe' and not o['dma']:
                    continue
                if d > best.get(p['eng'], -1):
                    best[p['eng']] = d
            o['wdeps'] = sorted(best.values()) + sorted(dmas)
            for d in best.values():
                signaling.add(d)
        last_on_eng = {}
        for o in ops:
            if not o['dma']:
                last_on_eng[o['eng']] = o['uid']
        for u in last_on_eng.values():
            signaling.add(u)
        first_seen = set()
        dma_events = []
        for o in ops:
            eng = o['eng']
            if eng not in first_seen:
                first_seen.add(eng)
                for ev in self.barrier_events:
                    self._wait(eng, ev)
            for d in o['wdeps']:
                self._wait(eng, self.events[d])
            if o['dma'] and eng == 'pool':
                sem = self.free_sems.pop()
                inst = o['fn'](self.engs[eng])
                inst.then_inc(sem, 16)
                ev = (sem, 16)
                self.events[o['uid']] = ev
                dma_events.append(ev)
            elif o['dma']:
                r = self.dnext
                self.dnext = (self.dnext + 1) % self.NDMA
                if self.dlast[r] is not None:
                    self._wait(eng, self.dlast[r])
                inst = o['fn'](self.engs[eng])
                self.dcount[r] += 16
                inst.then_inc(self.dsem[r], 16)
                ev = (self.dsem[r], self.dcount[r])
                self.dlast[r] = ev
                self.events[o['uid']] = ev
                dma_events.append(ev)
            else:
                inst = o['fn'](self.engs[eng])
                if o['uid'] in signaling:
                    c = self.ecount[eng]
                    k = c // self.CHUNK
                    while len(self.esem[eng]) <= k:
                        self.esem[eng].append(self.free_sems.pop())
                    sem = self.esem[eng][k]
                    inst.then_inc(sem, 1)
                    self.ecount[eng] = c + 1
                    self.events[o['uid']] = (sem, c - k * self.CHUNK + 1)
            self.n_inst += 1
        be = []
        for e, u in last_on_eng.items():
            be.append(self.events[u])
        seen = {}
        for ev in dma_events:
            seen[id(ev[0])] = ev
        be.extend(seen.values())
        self.barrier_events = be + [ev for ev in self.barrier_events
                                    if all(id(ev[0]) != id(b[0]) for b in be)]
        self.phase_start = self.uid

    def final_wait(self, eng='sp'):
        for ev in self.barrier_events:
            self._wait(eng, ev)


def I(name, *a, **k):
    return lambda e: getattr(e, name)(*a, **k)


def SEQ(*fs):
    def fn(e):
        inst = None
        for f in fs:
            inst = f(e)
        return inst
    return fn


def build_program(nc, n_layers=2):
    from contextlib import ExitStack

    def din(name, shape):
        return nc.dram_tensor(name, list(shape), F32, kind="ExternalInput").ap()

    xT_d = din("xT", [D, S])
    pT_d = din("pT", [2, PLE, S])
    vecs_d = din("vecs", [128, NVEC])
    cst_d = nc.dram_tensor("cst", [128, NCB * 128], BF16, kind="ExternalInput").ap()
    rc_d = din("rc", [128, 64])
    awin_d = din("a_w_in", [D, 2 * D])
    awg_d = din("a_w_group", [D, 256])
    awout_d = din("a_w_out", [D, D])
    wkv_d = din("w_kv_s", [16, D, 128])
    bwin_d = din("b_w_in_s", [16, D, 128])
    bwout_d = din("b_w_out", [D, D])
    plew_d = din("ple_w", [2, PLE, D])
    gate_d = din("ple_gate_w", [2, D, D])
    out_d = nc.dram_tensor("outT", [D, S], F32, kind="ExternalOutput").ap()
    dbg_d = nc.dram_tensor("dbgy", [D, S], F32, kind="ExternalOutput").ap() if DBG.get('dump') else None

    es = ExitStack()

    def sb(name, shape, dt):
        return es.enter_context(nc.sbuf_tensor(name, list(shape), dt))

    sems = [es.enter_context(nc.semaphore(f"s{i}")) for i in range(100)]
    P = Prog(nc, sems)

    xT = sb("xT_sb", [128, NC_, S], F32)
    vecs = sb("vecs_sb", [128, NVEC], F32)
    rcf = sb("rcf", [128, 64], F32)
    cstb = sb("cstb", [128, NCB * 128], BF16)
    zeros64 = sb("zeros64", [128, 64], BF16)
    qgain = sb("qgain", [128, 1], F32)
    sqb = [sb(f"sqb{i}", [128, TT], BF16) for i in range(2)]
    lnv = [sb(f"lnv{i}", [128, TT], F32) for i in range(2)]
    rstd = lnv
    ps = es.enter_context(nc.psum_tensor("ps", [128, 8, TT], F32))

    def cb(i):
        return cstb[:, i * 128:(i + 1) * 128]

    ident, tri, blk, ond, negp, negn, onef = [cb(i) for i in range(NCB)]
    rc = rcf[:, :]

    st_i = [0]
    stage = []

    def next_stage():
        r = st_i[0] % len(stage)
        st_i[0] += 1
        return r

    bank_i = [0]

    def next_bank():
        b = bank_i[0]
        bank_i[0] = (b + 1) % 8
        return b

    def xk(c, tt):
        return f'x{c}.{tt}'

    def xap(c, tt):
        return xT[:, c, tt * TT:(tt + 1) * TT]

    P.op('sp', I('dma_start', out=vecs[:], in_=vecs_d), writes=['vecs'], dma=True)
    P.op('sp', I('dma_start', out=cstb[:], in_=cst_d), writes=['cstb'], dma=True)
    P.op('sp', I('dma_start', out=rcf[:], in_=rc_d), writes=['cstf'], dma=True)
    P.op('pool', I('memset', zeros64[:], 0.0), writes=['zeros64'])
    P.op('pool', I('tensor_scalar', qgain[:], vecs[:, V_QN:V_QN + 1], SB_SCALE, 1.0, ALU.mult, ALU.mult),
         reads=['vecs'], writes=['qgain'])
    def load_x_tile(tt):
        for c in range(NC_):
            P.op('sp', I('dma_start', out=xap(c, tt), in_=xT_d[c * 128:(c + 1) * 128, tt * TT:(tt + 1) * TT]),
                 writes=[xk(c, tt)], dma=True)

    def load_w(dst, dkey, src2d, KC, N, gain_col0=None):
        W = stage[0].shape[-1]
        for kc in range(KC):
            for n0 in range(0, N, W):
                n1 = min(N, n0 + W)
                r = next_stage()
                P.op('sp', I('dma_start', out=stage[r][:, 0:n1 - n0], in_=src2d[kc * 128:(kc + 1) * 128, n0:n1]),
                     writes=[f'stage{r}'], dma=True)
                if gain_col0 is None:
                    P.op('pool', I('tensor_copy', dst[:, kc, n0:n1], stage[r][:, 0:n1 - n0]),
                         reads=[f'stage{r}'], writes=[f'{dkey}.{kc}'])
                else:
                    P.op('pool', I('tensor_scalar', dst[:, kc, n0:n1], stage[r][:, 0:n1 - n0],
                                   vecs[:, gain_col0 + kc:gain_col0 + kc + 1], 1.0, ALU.mult, ALU.mult),
                         reads=[f'stage{r}', 'vecs'], writes=[f'{dkey}.{kc}'])

    def mm_group(bank_ap, pairs, bkey, reads):
        n = len(pairs)
        fs = [I('matmul', bank_ap, l, r, start=(i == 0), stop=(i == n - 1)) for i, (l, r) in enumerate(pairs)]
        P.op('pe', SEQ(*fs), reads=reads, writes=[bkey])

    def rms_rstd(tt, slot):
        b = next_bank()
        for c in range(NC_):
            j = c % 2
            P.op('act', I('activation', out=sqb[j][:], in_=xap(c, tt), func=AF.Square),
                 reads=[xk(c, tt)], writes=[f'sqb{j}'])
            P.op('pe', I('matmul', ps[:, b, :], ond, sqb[j][:], start=(c == 0), stop=(c == NC_ - 1)),
                 reads=[f'sqb{j}', 'cstb'], writes=[f'ps{b}'])
        P.op('act', I('activation', out=lnv[slot][:], in_=ps[:, b, :], func=AF.Ln, bias=EPS),
             reads=[f'ps{b}'], writes=[f'lnv{slot}'])
        P.op('act', I('activation', out=rstd[slot][:], in_=lnv[slot][:], func=AF.Exp, scale=-0.5),
             reads=[f'lnv{slot}'], writes=[f'lnv{slot}'])

    def ple_phase(layer, Wple, Wgate, B):
        seen = set()

        def wk(k):
            if k in seen:
                return [k]
            seen.add(k)
            return [k] + B.get('alias', {}).get(k, [])
        def bufs(tt):
            alt = (tt % 2 == 1) and ('x1b2' in B)
            return ((B['x1b2'], B['x1b2k'], B['pb2'], B['pb2k']) if alt else (B['x1b'], B['x1bk'], B['pb'], B['pbk']))

        def prep(tt):
            x1b, x1bk, pbs, pbks = bufs(tt)
            t0 = tt * TT
            for kc in range(2):
                P.op('pool', I('dma_start', out=pbs[kc], in_=pT_d[layer, kc * 128:(kc + 1) * 128, t0:t0 + TT]),
                     writes=wk(pbks[kc]), dma=True)
            for c in range(NC_):
                P.op('act', I('activation', out=x1b[c], in_=xap(c, tt), func=AF.Copy),
                     reads=[xk(c, tt)], writes=wk(x1bk[c]))

        prep(0)
        for tt in range(NTT):
            t0 = tt * TT
            x1b, x1bk, pbs, pbks = bufs(tt)
            if tt + 1 < NTT and 'x1b2' in B:
                prep(tt + 1)
            for n in range(NC_):
                ba = next_bank()
                bg = next_bank()
                mm_group(ps[:, ba, :], [(Wple[:, kc, n * 128:(n + 1) * 128], pbs[kc]) for kc in range(2)],
                         f'ps{ba}', B['wplek'] + pbks)
                mm_group(ps[:, bg, :], [(Wgate[:, kc, n * 128:(n + 1) * 128], x1b[kc]) for kc in range(NC_)],
                         f'ps{bg}', B['wgatek'] + x1bk)
                j = n % 2
                P.op('act', I('activation', out=B['sg'][j], in_=ps[:, bg, :], func=AF.Sigmoid),
                     reads=[f'ps{bg}'], writes=wk(B['sgk'][j]))
                P.op('dve', I('tensor_tensor', B['tm'][j], ps[:, ba, :], B['sg'][j], ALU.mult),
                     reads=[f'ps{ba}', B['sgk'][j]], writes=wk(B['tmk'][j]))
                P.op('pool', I('tensor_tensor', xap(n, tt), xap(n, tt), B['tm'][j], ALU.add),
                     reads=[B['tmk'][j], xk(n, tt)], writes=[xk(n, tt)])
                if B.get('store'):
                    P.op('sp', I('dma_start', out=out_d[n * 128:(n + 1) * 128, t0:t0 + TT], in_=xap(n, tt)),
                         reads=[xk(n, tt)], writes=[f'out.{n}.{tt}'], dma=True)
            if tt + 1 < NTT and 'x1b2' not in B:
                prep(tt + 1)

    def outproj_phase(Wout, y_ap_fn, ykeys_fn, tts):
        for tt in tts:
            for n in range(NC_):
                b = next_bank()
                mm_group(ps[:, b, :], [(Wout[:, dc, n * 128:(n + 1) * 128], y_ap_fn(dc, tt)) for dc in range(NC_)],
                         f'ps{b}', [f'wout.{k}' for k in range(NC_)] + [ykeys_fn(k, tt) for k in range(NC_)])
                P.op('dve', I('tensor_tensor', xap(n, tt), ps[:, b, :], xap(n, tt), ALU.add),
                     reads=[f'ps{b}', xk(n, tt)], writes=[xk(n, tt)])

    with ExitStack() as l0:
        def sb0(name, shape, dt):
            return l0.enter_context(nc.sbuf_tensor(name, list(shape), dt))
        Win = sb0("Win", [128, NC_, 2 * D], BF16)
        Wg = sb0("Wg", [128, 8, 256], BF16)
        Wout = sb0("Wout0", [128, NC_, D], BF16)
        xh = sb0("xh", [128, NC_, TT], BF16)
        ubuf = sb0("ubuf", [128, NC_, 16 + TT], F32)
        sAB = [sb0(f"sAB{i}", [128, 16 + TT], F32) for i in range(4)]
        pooled = [sb0(f"pooled{i}", [128, 2, TT], BF16) for i in range(2)]
        sz = [sb0(f"sz{i}", [128, 2, TT], BF16) for i in range(2)]
        y2 = [sb0(f"y0_{i}", [128, NC_, TT], BF16) for i in range(2)]
        fixt = sb0("fixt", [128, 16], F32)

        Wple0 = sb0("Wple0", [128, 2, D], BF16)
        Wgate0 = sb0("Wgate0", [128, NC_, D], BF16)

        def cast_dma(dst, dkeys, srcv):
            P.op('pool', I('dma_start', out=dst, in_=srcv), writes=dkeys, dma=True)

        def load_win_group(g):
            for h in range(2):
                c0 = h * D + g * 256
                srcv = awin_d.rearrange("(kc p) n -> p kc n", p=128)[:, :, c0:c0 + 256]
                cast_dma(Win[:, :, c0:c0 + 256], [f'win.g{g}'], srcv)

        P.op('pool', I('memset', ubuf[:], 0.0), writes=[f'ubuf{c}' for c in range(NC_)])
        load_x_tile(0)
        load_win_group(0)
        load_win_group(1)
        load_x_tile(1)
        load_win_group(2)
        load_win_group(3)
        cast_dma(Wg[:], [f'wg.{k}' for k in range(8)], awg_d.rearrange("(kc p) n -> p kc n", p=128))
        cast_dma(Wout[:], [f'wout.{k}' for k in range(NC_)], awout_d.rearrange("(kc p) n -> p kc n", p=128))
        load_x_tile(2)
        load_x_tile(3)
        cast_dma(Wple0[:], ['wple.0', 'wple.1'], plew_d[0].rearrange("(kc p) n -> p kc n", p=128))
        cast_dma(Wgate0[:], [f'wgate.{k}' for k in range(NC_)], gate_d[0].rearrange("(kc p) n -> p kc n", p=128))

        L = 16 + TT
        xh_keys = [f'xh.{k}' for k in range(NC_)]

        def xhmul(tt):
            slot = tt % 2
            for c in range(NC_):
                P.op('dve', I('scalar_tensor_tensor', xh[:, c, :], xap(c, tt), vecs[:, V_ANORM + c:V_ANORM + c + 1],
                              rstd[slot][:], ALU.mult, ALU.mult),
                     reads=[xk(c, tt), f'lnv{slot}', 'vecs'], writes=[f'xh.{c}'])

        def outproj_chunks(tt, ns):
            yb = tt % 2
            for n in ns:
                b = next_bank()
                mm_group(ps[:, b, :], [(Wout[:, dc, n * 128:(n + 1) * 128], y2[yb][:, dc, :]) for dc in range(NC_)],
                         f'ps{b}', [f'wout.{k}' for k in range(NC_)] + [f'y{yb}.{k}' for k in range(NC_)])
                P.op('dve', I('tensor_tensor', xap(n, tt), ps[:, b, :], xap(n, tt), ALU.add),
                     reads=[f'ps{b}', xk(n, tt)], writes=[xk(n, tt)])

        osched = {0: [2, 3], 1: [4, 5], 2: [6, 7], 3: []}
        rms_rstd(0, 0)
        xhmul(0)
        for tt in range(NTT):
            y = y2[tt % 2]
            ykey = f'y{tt % 2}'
            def front(g):
                w = POOL_W[g]
                pj = g % 2
                for cl in range(2):
                    c = 2 * g + cl
                    bu = next_bank()
                    mm_group(ps[:, bu, :], [(Win[:, kc, c * 128:(c + 1) * 128], xh[:, kc, :]) for kc in range(NC_)],
                             f'ps{bu}', xh_keys + [f'win.g{g}'])
                    P.op('act', I('activation', out=ubuf[:, c, 16:16 + TT], in_=ps[:, bu, :], func=AF.Copy),
                         reads=[f'ps{bu}'], writes=[f'ubuf{c}'])
                    bz = next_bank()
                    mm_group(ps[:, bz, :], [(Win[:, kc, D + c * 128:D + (c + 1) * 128], xh[:, kc, :]) for kc in range(NC_)],
                             f'ps{bz}', xh_keys + [f'win.g{g}'])
                    P.op('act', I('activation', out=sz[pj][:, cl, :], in_=ps[:, bz, :], func=AF.Silu),
                         reads=[f'ps{bz}'], writes=[f'sz{pj}.{cl}'])
                    src = ubuf[:, c, :]
                    skey = f'ubuf{c}'
                    bufs = [sAB[2 * cl], sAB[2 * cl + 1]]
                    bkeys = [f'sAB{2 * cl}', f'sAB{2 * cl + 1}']
                    step, lo, bi = 1, 0, 0
                    while step < w:
                        lo2 = lo + step
                        dst = bufs[bi]
                        P.op('dve', I('tensor_tensor', dst[:, lo2:L], src[:, lo2:L], src[:, lo2 - step:L - step], ALU.add),
                             reads=[skey], writes=[bkeys[bi]])
                        src = dst[:, :]
                        skey = bkeys[bi]
                        bi ^= 1
                        lo = lo2
                        step *= 2
                    P.op('dve', I('scalar_tensor_tensor', pooled[pj][:, cl, :], src[:, 16:16 + TT], 1.0 / w,
                                  ubuf[:, c, 16:16 + TT], ALU.mult, ALU.subtract),
                         reads=[skey, f'ubuf{c}'], writes=[f'pooled{pj}.{cl}'])
                    if tt == 0:
                        P.op('dve', I('tensor_tensor', fixt[:, 0:w - 1], src[:, 16:16 + w - 1],
                                      rc[:, g * 16:g * 16 + w - 1], ALU.mult),
                             reads=[skey, 'cstf'], writes=['fixt'])
                        P.op('dve', I('tensor_tensor', pooled[pj][:, cl, 0:w - 1], fixt[:, 0:w - 1],
                                      ubuf[:, c, 16:16 + w - 1], ALU.subtract),
                             reads=['fixt', f'ubuf{c}'], writes=[f'pooled{pj}.{cl}'])
                    P.op('pool', I('tensor_copy', ubuf[:, c, 0:16], ubuf[:, c, TT:TT + 16]),
                         reads=[f'ubuf{c}'], writes=[f'ubuf{c}'])

            def back(g):
                pj = g % 2
                if tt > 0:
                    outproj_chunks(tt - 1, osched[g])
                for dl in range(2):
                    dc = 2 * g + dl
                    bm = next_bank()
                    mm_group(ps[:, bm, :], [(Wg[:, 2 * g + kc, dl * 128:(dl + 1) * 128], pooled[pj][:, kc, :]) for kc in range(2)],
                             f'ps{bm}', [f'wg.{2 * g}', f'wg.{2 * g + 1}', f'pooled{pj}.0', f'pooled{pj}.1'])
                    P.op('dve', I('scalar_tensor_tensor', y[:, dc, :], ps[:, bm, :],
                                  vecs[:, V_ASCALE + dc:V_ASCALE + dc + 1], sz[pj][:, dl, :], ALU.mult, ALU.mult),
                         reads=[f'ps{bm}', 'vecs', f'sz{pj}.{dl}'], writes=[f'{ykey}.{dc}'])
                if g == 1 and tt + 1 < NTT:
                    rms_rstd(tt + 1, (tt + 1) % 2)

            front(0)
            for g in range(4):
                if g + 1 < 4:
                    front(g + 1)
                if g == 2 and tt + 1 < NTT:
                    xhmul(tt + 1)
                back(g)
            outproj_chunks(tt, [0, 1])
        outproj_chunks(NTT - 1, [2, 3, 4, 5, 6, 7])
        B0 = dict(x1b=[xh[:, c, :] for c in range(NC_)], x1bk=[f'xh.{c}' for c in range(NC_)],
                  pst=[ubuf[:, k, 16:16 + TT] for k in range(2)], pstk=['ubuf0', 'ubuf1'],
                  pb=[pooled[0][:, k, :] for k in range(2)], pbk=['pooled0.0', 'pooled0.1'],
                  x1b2=[y2[0][:, c, :] for c in range(NC_)], x1b2k=[f'y0.{c}' for c in range(NC_)],
                  pb2=[pooled[1][:, k, :] for k in range(2)], pb2k=['pooled1.0', 'pooled1.1'],
                  sg=[sAB[0][:, 0:TT], sAB[1][:, 0:TT]], sgk=['sAB0', 'sAB1'],
                  tm=[sAB[2][:, 0:TT], sAB[3][:, 0:TT]], tmk=['sAB2', 'sAB3'],
                  wplek=['wple.0', 'wple.1'], wgatek=[f'wgate.{k}' for k in range(NC_)])
        ple_phase(0, Wple0, Wgate0, B0)
        P.emit()

    def ple_scope(layer):
        with ExitStack() as l:
            def sbl(name, shape, dt):
                return l.enter_context(nc.sbuf_tensor(name, list(shape), dt))
            stage[:] = [sbl(f"stagep{layer}_{i}", [128, D], F32) for i in range(2)]
            Wple = sbl(f"Wple{layer}", [128, 2, D], BF16)
            Wgate = sbl(f"Wgate{layer}", [128, NC_, D], BF16)
            x1b = sbl(f"x1b{layer}", [128, NC_, TT], BF16)
            pstage = sbl(f"pstage{layer}", [128, 2, TT], F32)
            pb = sbl(f"pb{layer}", [128, 2, TT], BF16)
            sg = [sbl(f"sg{layer}{i}", [128, TT], F32) for i in range(2)]
            tm = [sbl(f"tm{layer}{i}", [128, TT], F32) for i in range(2)]
            load_w(Wple, 'wple', plew_d[layer], 2, D)
            load_w(Wgate, 'wgate', gate_d[layer], NC_, D)
            B = dict(x1b=[x1b[:, c, :] for c in range(NC_)], x1bk=[f'x1b.{c}' for c in range(NC_)],
                     pst=[pstage[:, k, :] for k in range(2)], pstk=['pstage.0', 'pstage.1'],
                     pb=[pb[:, k, :] for k in range(2)], pbk=['pb.0', 'pb.1'],
                     sg=[sg[0][:], sg[1][:]], sgk=['sg0', 'sg1'], tm=[tm[0][:], tm[1][:]], tmk=['tm0', 'tm1'],
                     wplek=['wple.0', 'wple.1'], wgatek=[f'wgate.{k}' for k in range(NC_)])
            ple_phase(layer, Wple, Wgate, B)
            P.emit()


    if n_layers >= 2:
        with ExitStack() as l1:
            y1 = l1.enter_context(nc.sbuf_tensor("y1", [128, NC_, S], BF16))
            with ExitStack() as l1a:
                def sba(name, shape, dt):
                    return l1a.enter_context(nc.sbuf_tensor(name, list(shape), dt))
                xh1 = sba("xh1", [128, NC_, S], BF16)
                KT2 = [sba(f"KT{i}", [128, S], BF16) for i in range(2)]
                QT2 = [sba(f"QT{i}", [128, S], BF16) for i in range(2)]
                Vt2 = [sba(f"Vt{i}", [128, 16, 128], BF16) for i in range(2)]
                szc2 = [sba("szc0", [128, S], BF16)] * 2
                wsl = {k: sba(f"wsl_{k}", [128, NC_, 128], BF16) for k in ('k', 'v', 'q', 'z')}
                stg1 = [sba("stg1_0", [128, NC_, 128], F32)] * 2
                zeros128 = sba("zeros128", [128, 128], BF16)
                ebuf = [sba(f"ebuf{i}", [128, 2, TT], F32) for i in range(3)]
                spb = [sba(f"spb{i}", [128, 2, TT], BF16) for i in range(2)]
                gbuf = [sba(f"gbuf{i}", [128, 2, TT], F32) for i in range(2)]
                abuf = [sba("abuf0", [128, 2, TT], BF16)] * 2
                rrow2 = sba("rrow2", [64, TT], BF16)
                LB, OB = 0, 4
                spare = [5, 6]
                sp_i = [0]

                owned = set()

                def spare_bank():
                    b = spare[sp_i[0] % len(spare)]
                    sp_i[0] += 1
                    assert b not in owned, f"spare bank {b} still owned"
                    owned.add(b)
                    return b

                def release(b):
                    owned.discard(b)

                P.op('pool', I('memset', zeros128[:], 0.0), writes=['zeros128'])

                st1_i = [0]

                def proj_items(cc, which):
                    bs = cc % 2
                    KT, QT, Vt, szc = KT2[bs], QT2[bs], Vt2[bs], szc2[bs]
                    items = []

                    def w_item(nm, src, col0):
                        def st0():
                            P.op('sp', I('dma_start', out=stg1[0][:], in_=src.rearrange("(kc p) n -> p kc n", p=128)),
                                 writes=['stg1'], dma=True)

                        def cast(kcs):
                            def f():
                                for kc in kcs:
                                    P.op('pool', I('tensor_scalar', wsl[nm][:, kc, :], stg1[0][:, kc, :],
                                                   vecs[:, col0 + kc:col0 + kc + 1], 1.0, ALU.mult, ALU.mult),
                                         reads=['stg1', 'vecs'], writes=[f'wsl{nm}'])
                            return f
                        if 'W' in which:
                            return [st0, lambda: None, cast([0, 1, 2, 3]), cast([4, 5, 6, 7])]
                        return [lambda: (st0(), cast(range(NC_))())]

                    def kq_item(nm, dst, dkey, gain_ap, tt):
                        st = {}

                        def st0():
                            st['b'] = spare_bank()
                            b = st['b']
                            mm_group(ps[:, b, :], [(wsl[nm][:, kc, :], xh1[:, kc, tt * TT:(tt + 1) * TT]) for kc in range(NC_)],
                                     f'ps{b}', [f'wsl{nm}'] + [f'xh1.{k}.{tt}' for k in range(NC_)])

                        def st1():
                            b = st['b']
                            j = tt % 2
                            st['j'] = j
                            P.op('act', I('activation', out=sqb[j][:], in_=ps[:, b, :], func=AF.Square),
                                 reads=[f'ps{b}'], writes=[f'sqb{j}'])
                            st['b2'] = spare_bank()
                            b2 = st['b2']
                            P.op('pe', I('matmul', ps[:, b2, :], blk, sqb[j][:], start=True, stop=True),
                                 reads=[f'sqb{j}', 'cstb'], writes=[f'ps{b2}'])

                        def st2():
                            b, b2, j = st['b'], st['b2'], st['j']
                            P.op('act', I('activation', out=lnv[j][:], in_=ps[:, b2, :], func=AF.Ln, bias=EPS),
                                 reads=[f'ps{b2}'], writes=[f'lnv{j}'])
                            P.op('act', I('activation', out=rstd[j][:], in_=lnv[j][:], func=AF.Exp, scale=-0.5),
                                 reads=[f'lnv{j}'], writes=[f'lnv{j}'])
                            P.op('dve', I('scalar_tensor_tensor', dst[:, tt * TT:(tt + 1) * TT], ps[:, b, :], gain_ap,
                                          rstd[j][:], ALU.mult, ALU.mult),
                                 reads=[f'ps{b}', f'lnv{j}', 'vecs', 'qgain'], writes=[f'{dkey}{bs}.{tt}'])
                            release(b)
                            release(b2)
                        return [st0, st1, st2]

                    def v_item(bk):
                        st = {}

                        def st0():
                            st['b'] = spare_bank()
                            b = st['b']
                            mm_group(ps[:, b, 0:128], [(xh1[:, kc, bk * 128:(bk + 1) * 128], wsl['v'][:, kc, :]) for kc in range(NC_)],
                                     f'ps{b}', ['wslv'] + [f'xh1.{k}.{bk // 4}' for k in range(NC_)])

                        def st1():
                            b = st['b']
                            P.op('dve', I('tensor_copy', Vt[:, bk, :], ps[:, b, 0:128]),
                                 reads=[f'ps{b}'], writes=[f'Vt{bs}.{bk}'])
                            release(b)
                        return [st0, st1]

                    def z_item(tt):
                        st = {}

                        def st0():
                            st['b'] = spare_bank()
                            b = st['b']
                            mm_group(ps[:, b, :], [(wsl['z'][:, kc, :], xh1[:, kc, tt * TT:(tt + 1) * TT]) for kc in range(NC_)],
                                     f'ps{b}', ['wslz'] + [f'xh1.{k}.{tt}' for k in range(NC_)])

                        def st1():
                            b = st['b']
                            P.op('act', I('activation', out=lnv[0][:], in_=ps[:, b, :], func=AF.Exp, scale=-1.0),
                                 reads=[f'ps{b}'], writes=['lnv0'])
                            P.op('act', I('activation', out=lnv[0][:], in_=lnv[0][:], func=AF.Ln, bias=1.0),
                                 reads=['lnv0'], writes=['lnv0'])
                            P.op('act', I('activation', out=lnv[0][:], in_=lnv[0][:], func=AF.Exp, scale=-1.0),
                                 reads=['lnv0'], writes=['lnv0'])
                            P.op('dve', I('tensor_tensor', szc[:, tt * TT:(tt + 1) * TT], ps[:, b, :], lnv[0][:], ALU.mult),
                                 reads=['lnv0', f'ps{b}'], writes=[f'szc.{tt}'])
                            release(b)
                        return [st0, st1]

                    if 'w' in which or 'W' in which:
                        for wi in (w_item('k', wkv_d[cc], V_KVN), w_item('q', bwin_d[cc], V_BN),
                                   w_item('v', wkv_d[8 + cc], V_KVN), w_item('z', bwin_d[8 + cc], V_BN)):
                            items.append(wi)
                            if 'W' in which:
                                items.extend([[] for _ in range(3)])
                    kq = []
                    if 'k' in which or 'K' in which:
                        for tt in range(NTT):
                            kq.append(kq_item('k', KT, 'KT', vecs[:, V_KN:V_KN + 1], tt))
                    if 'k' in which or 'Q' in which:
                        for tt in range(NTT):
                            kq.append(kq_item('q', QT, 'QT', qgain[:], tt))
                    if 'K' in which or 'Q' in which:
                        return kq
                    vz = []
                    if 'v' in which:
                        for bk in range(16):
                            vz.append(v_item(bk))
                    if 'z' in which:
                        for tt in range(NTT):
                            vz.append(z_item(tt))
                    if which in ('v', 'z'):
                        return vz
                    while kq or vz:
                        if kq:
                            items.append(kq.pop(0))
                        for _ in range(2):
                            if vz:
                                items.append(vz.pop(0))
                    return items

                def run_stages(items, i, order=(5, 4, 3, 2, 1, 0)):
                    for s in order:
                        k = i - s
                        if 0 <= k < len(items) and s < len(items[k]):
                            items[k][s]()

                spare[:] = [5, 6, 7, 0, 1, 2, 3]
                itw = proj_items(0, 'w')
                for i in range(len(itw) + 6):
                    run_stages(itw, i)
                for tt in range(NTT):
                    slot = tt % 2
                    rms_rstd(tt, slot)
                    for c in range(NC_):
                        P.op('dve', I('tensor_tensor', xh1[:, c, tt * TT:(tt + 1) * TT], xap(c, tt), rstd[slot][:], ALU.mult),
                             reads=[xk(c, tt), f'lnv{slot}'], writes=[f'xh1.{c}.{tt}'])
                it0 = proj_items(0, 'vkz')
                for i in range(len(it0) + 6):
                    run_stages(it0, i, order=(0, 1, 2, 3, 4, 5))

                Wout1v = xh1[:, 0:4, :].rearrange("p a (b n) -> p (a b) n", b=2)
                Wgate1v = xh1[:, 4:8, :].rearrange("p a (b n) -> p (a b) n", b=2)
                Wple1v = KT2[0][:, :].rearrange("p (a n) -> p a n", a=2)
                stgA = stg1[0][:].rearrange("p kc n -> p (kc n)")
                stgB = QT2[0].bitcast(F32)
                xh1_lo = [f'xh1.{cc}.{t}' for cc in range(0, 4) for t in range(NTT)]
                xh1_hi = [f'xh1.{cc}.{t}' for cc in range(4, 8) for t in range(NTT)]
                kt0_keys = [f'KT0.{t}' for t in range(NTT)]
                qt0_keys = [f'QT0.{t}' for t in range(NTT)]

                def end_items():
                    def st0():
                        P.op('pool', I('dma_start', out=Wout1v, in_=bwout_d.rearrange("(kc p) n -> p kc n", p=128)),
                             writes=[f'wout.{k}' for k in range(NC_)] + xh1_lo, dma=True)
                        P.op('pool', I('dma_start', out=Wgate1v, in_=gate_d[1].rearrange("(kc p) n -> p kc n", p=128)),
                             writes=[f'wgate.{k}' for k in range(NC_)] + xh1_hi, dma=True)
                        P.op('pool', I('dma_start', out=Wple1v, in_=plew_d[1].rearrange("(kc p) n -> p kc n", p=128)),
                             writes=['wple.0', 'wple.1'] + kt0_keys, dma=True)
                    return [[], [], [st0]]

                spare[:] = [7]
                sp_i[0] = 0
                def outproj_item(n, tt, dcs):
                    st = {}

                    def st0():
                        st['b'] = spare_bank()
                        b = st['b']
                        mm_group(ps[:, b, :], [(Wout1v[:, dc, n * 128:(n + 1) * 128], y1[:, dc, tt * TT:(tt + 1) * TT]) for dc in dcs],
                                 f'ps{b}', [f'wout.{k}' for k in dcs] + [f'y1.{k}.{tt}' for k in dcs])

                    def st1():
                        b = st['b']
                        P.op('dve', I('tensor_tensor', xap(n, tt), ps[:, b, :], xap(n, tt), ALU.add),
                             reads=[f'ps{b}', xk(n, tt)], writes=[xk(n, tt)])
                        release(b)
                    return [st0, st1]

                tile_n = 0
                for c in range(DBG['nchunks']):
                    bs = c % 2
                    KT, QT, Vt, szc = KT2[bs], QT2[bs], Vt2[bs], szc2[bs]
                    if c + 1 < DBG['nchunks']:
                        nxt = proj_items(c + 1, 'W')
                        kk = proj_items(c + 1, 'K')
                        qq = proj_items(c + 1, 'Q')
                        vv = proj_items(c + 1, 'v')
                        zz = proj_items(c + 1, 'z')
                        nxtP = [[] for _ in range(41)]
                        for t_, it in enumerate(vv):
                            nxtP[14 + t_] = it
                        for t_, it in enumerate(zz[:3]):
                            nxtP[30 + t_] = it
                        nxtP[40] = zz[3]
                        for t_, it in enumerate(kk):
                            nxtP[4 + t_] = it
                        for t_, it in enumerate(qq):
                            nxtP[8 + t_] = it
                        kq_next = []
                    else:
                        nxt = end_items()
                        kq_next = []
                        nxtP = [[] for _ in range(10)]
                        for tt_ in range(NTT):
                            for n_ in range(NC_):
                                nxtP.append(outproj_item(n_, tt_, list(range(NC_ - 1))))
                    seq = []
                    for qt in range(NTT):
                        for kb in range(4 * qt + 3, -1, -1):
                            seq.append((qt, kb))
                    if DBG['ntiles'] is not None:
                        seq = seq[:DBG['ntiles']]
                    NS = len(seq)

                    KQW = 14

                    def cb_of(i):
                        if i < KQW:
                            return 2
                        return 5 if (i - KQW) % 2 == 0 else 2

                    def tinfo(i):
                        qt, kb = seq[i]
                        j = kb - 4 * qt
                        lo = 128 * j if j > 0 else 0
                        first = (kb == 4 * qt + 3)
                        if first:
                            lor = None
                        else:
                            jn = j + 1
                            lor = 128 * jn + 1 if jn >= 0 else 0
                        return qt, kb, j, lo, first, lor, tile_n + i

                    def rec_qk(i):
                        qt, kb, j, lo, first, lor, n = tinfo(i)
                        t0 = qt * TT
                        fs = []
                        for h in range(2):
                            p0 = 64 * h
                            fs.append(I('matmul', ps[:, LB + h, lo:TT], KT[p0:p0 + 64, kb * 128:(kb + 1) * 128],
                                        QT[p0:p0 + 64, t0 + lo:t0 + TT], start=True, stop=(j < 0)))
                        if j >= 0:
                            for h in range(2):
                                fs.append(I('matmul', ps[:, LB + h, lo:lo + 128], ident, negn, start=False, stop=True))
                        P.op('pe', SEQ(*fs), reads=[f'KT{bs}.{kb // 4}', f'QT{bs}.{qt}', 'cstb'],
                             writes=[f'ps{LB}', f'ps{LB + 1}'])

                    def rec_p1(i):
                        qt, kb, j, lo, first, lor, n = tinfo(i)
                        eb = n % 3
                        P.op('act', I('activation', out=ebuf[eb][:, :, lo:TT], in_=ps[:, LB:LB + 2, lo:TT], func=AF.Exp),
                             reads=[f'ps{LB}', f'ps{LB + 1}'], writes=[f'ebuf{eb}'])

                    def rec_p2(i):
                        qt, kb, j, lo, first, lor, n = tinfo(i)
                        eb = n % 3
                        sj = n % 2
                        P.op('act', I('activation', out=spb[sj][:, :, lo:TT], in_=ebuf[eb][:, :, lo:TT], func=AF.Ln, bias=1.0),
                             reads=[f'ebuf{eb}'], writes=[f'spb{sj}'])

                    def rec_tri(i):
                        qt, kb, j, lo, first, lor, n = tinfo(i)
                        CB = cb_of(i)
                        sj = n % 2
                        rj = n % 2
                        fs = []
                        for h in range(2):
                            fs.append(I('matmul', ps[:, CB + h, lo:TT], tri, spb[sj][:, h, lo:TT], start=True, stop=(lor is None)))
                        P.op('pe', SEQ(*fs), reads=[f'spb{sj}', 'cstb'], writes=[f'ps{CB}', f'ps{CB + 1}'])
                        if lor is not None:
                            fs = []
                            for h in range(2):
                                fs.append(I('matmul', ps[:, CB + h, lor:TT], onef[32 * h:32 * h + 1, :],
                                            rrow2[32 * h:32 * h + 1, lor:TT], start=False, stop=True))
                            P.op('pe', SEQ(*fs), reads=['rrow', 'cstb'], writes=[f'ps{CB}', f'ps{CB + 1}'])

                    def rec_rcopy(i):
                        qt, kb, j, lo, first, lor, n = tinfo(i)
                        CB = cb_of(i)
                        if kb > 0:
                            lorn = 128 * j + 1 if j >= 0 else 0
                            for h in range(2):
                                P.op('dve', I('tensor_copy', rrow2[32 * h:32 * h + 1, lorn:TT], ps[0:1, CB + h, lorn:TT]),
                                     reads=[f'ps{CB + h}'], writes=['rrow'])

                    def rec_p3(i):
                        qt, kb, j, lo, first, lor, n = tinfo(i)
                        CB = cb_of(i)
                        gj = n % 2
                        eb = n % 3
                        P.op('act', I('activation', out=gbuf[gj][:, :, lo:TT], in_=ps[:, CB:CB + 2, lo:TT], func=AF.Exp, scale=-1.0),
                             reads=[f'ps{CB}', f'ps{CB + 1}'], writes=[f'gbuf{gj}'])
                        rec_rcopy(i)
                        P.op('dve', I('tensor_tensor', abuf[0][:, 0, lo:TT], ebuf[eb][:, 0, lo:TT], gbuf[gj][:, 0, lo:TT], ALU.mult),
                             reads=[f'ebuf{eb}', f'gbuf{gj}'], writes=['abuf.0'])
                        P.op('pool', I('tensor_tensor', abuf[0][:, 1, lo:TT], ebuf[eb][:, 1, lo:TT], gbuf[gj][:, 1, lo:TT], ALU.mult),
                             reads=[f'ebuf{eb}', f'gbuf{gj}'], writes=['abuf.1'])

                    def rec_av(i):
                        qt, kb, j, lo, first, lor, n = tinfo(i)
                        aj = n % 2
                        fs = []
                        if first:
                            fs.append(I('matmul', ps[:, OB, :], zeros128[:], KT[:, 0:TT], start=True, stop=False))
                        for h in range(2):
                            p0 = 64 * h
                            fs.append(I('matmul', ps[p0:p0 + 64, OB, lo:TT], Vt[:, kb, p0:p0 + 64], abuf[aj][:, h, lo:TT],
                                        start=False, stop=(kb == 0)))
                        P.op('pe', SEQ(*fs), reads=['abuf.0', 'abuf.1', f'Vt{bs}.{kb}', 'zeros128', f'KT{bs}.0'],
                             writes=[f'ps{OB}'])
                        if kb == 0:
                            t0 = qt * TT
                            P.op('dve', I('tensor_tensor', y1[:, c, t0:t0 + TT], ps[:, OB, :], szc[:, t0:t0 + TT], ALU.mult),
                                 reads=[f'ps{OB}', f'szc.{qt}'], writes=[f'y1.{c}.{qt}'])

                    if NS > 0:
                        rec_qk(0)
                    nsteps = max(NS + 1 if NS > 0 else 0, len(nxt) + 6 if nxt else 0, len(nxtP) + 6 if nxtP else 0)
                    for i in range(nsteps):
                        if i < NS:
                            rec_p1(i)
                        if i + 1 < NS:
                            rec_qk(i + 1)
                        spare[:] = [7, 5, 6] if i < KQW else [7]
                        run_stages(nxtP, i)
                        if i < NS:
                            light = i < len(nxtP) and len(nxtP[i]) == 2
                            heavy = i < len(nxtP) and len(nxtP[i]) == 3
                            nd = 0 if heavy else (2 if light else 4)
                            o_open = i >= 2 and not tinfo(i - 1)[4] and not tinfo(i - 2)[1] == 0
                            if nd and o_open and DBG.get('dummy', 1):
                                fs = [I('matmul', ps[:, OB, :], zeros128[:], KT[:, 0:TT], start=False, stop=False) for _ in range(nd)]
                                P.op('pe', SEQ(*fs), reads=[f'KT{bs}.0', 'zeros128'], writes=[f'ps{OB}'])
                        if i < NS:
                            rec_p2(i)
                        if 1 <= i <= NS:
                            rec_p3(i - 1)
                        if i < NS:
                            rec_tri(i)
                        run_stages(nxt, i)
                        if 1 <= i <= NS:
                            rec_av(i - 1)
                    tile_n += NS
                    if kq_next:
                        assert not owned, owned
                        spare[:] = [7, 0, 1, 5, 6, 2, 3]
                        for ii in range(len(kq_next) + 6):
                            run_stages(kq_next, ii, order=(0, 1, 2, 3, 4, 5))
                        assert not owned, owned
                        spare[:] = [7]

                last = NC_ - 1 if DBG['nchunks'] == NC_ else None
                for tt in range(NTT):
                    for n in range(NC_):
                        b = next_bank()
                        dcs = [last] if last is not None else list(range(NC_))
                        mm_group(ps[:, b, :], [(Wout1v[:, dc, n * 128:(n + 1) * 128], y1[:, dc, tt * TT:(tt + 1) * TT]) for dc in dcs],
                                 f'ps{b}', [f'wout.{k}' for k in dcs] + [f'y1.{k}.{tt}' for k in dcs])
                        P.op('dve', I('tensor_tensor', xap(n, tt), ps[:, b, :], xap(n, tt), ALU.add),
                             reads=[f'ps{b}', xk(n, tt)], writes=[xk(n, tt)])
                B1 = dict(x1b=[szc2[0][:, cc * TT:(cc + 1) * TT] for cc in range(4)] + [spb[i][:, h, :] for i in range(2) for h in range(2)],
                          x1bk=[f'szc.{cc}' for cc in range(4)] + ['x1b.4', 'x1b.5', 'x1b.6', 'x1b.7'],
                          pst=[ebuf[0][:, k, :] for k in range(2)], pstk=['pst.0', 'pst.1'],
                          alias={'x1b.4': ['spb0'], 'x1b.5': ['spb0'], 'x1b.6': ['spb1'], 'x1b.7': ['spb1'],
                                 'pst.0': ['ebuf0'], 'pst.1': ['ebuf0'], 'pb2.0': ['gbuf0'], 'pb2.1': ['gbuf0']},
                          pb=[abuf[0][:, k, :] for k in range(2)], pbk=['abuf.0', 'abuf.1'],
                          x1b2=[KT2[1][:, cc * TT:(cc + 1) * TT] for cc in range(4)] + [QT2[1][:, cc * TT:(cc + 1) * TT] for cc in range(4)],
                          x1b2k=[f'KT1.{cc}' for cc in range(4)] + [f'QT1.{cc}' for cc in range(4)],
                          pb2=[gbuf[0].bitcast(BF16)[:, 1, k * TT:(k + 1) * TT] for k in range(2)],
                          pb2k=['pb2.0', 'pb2.1'],
                          sg=[lnv[0][:], lnv[1][:]], sgk=['lnv0', 'lnv1'],
                          tm=[gbuf[0][:, 0, :], gbuf[1][:, 0, :]], tmk=['gbuf0', 'gbuf1'],
                          wplek=['wple.0', 'wple.1'], wgatek=[f'wgate.{k}' for k in range(NC_)], store=True)
                ple_phase(1, Wple1v, Wgate1v, B1)
                P.emit()

    if n_layers < 2:
        for c in range(NC_):
            P.op('sp', I('dma_start', out=out_d[c * 128:(c + 1) * 128, :], in_=xT[:, c, :]),
                 reads=[xk(c, t) for t in range(NTT)], dma=True)
        P.emit()
    P.final_wait('sp')
    es.close()
    return P


def make_consts():
    c = np.zeros((128, NCB * 128), np.float32)
    idx = np.arange(128)
    c[:, C_ID * 128:(C_ID + 1) * 128] = np.eye(128, dtype=np.float32)
    c[:, C_TRI * 128:(C_TRI + 1) * 128] = (idx[:, None] >= idx[None, :]).astype(np.float32)
    c[:, C_BLK * 128:(C_BLK + 1) * 128] = ((idx[:, None] // 64) == (idx[None, :] // 64)).astype(np.float32) / 64.0
    c[:, C_OND * 128:(C_OND + 1) * 128] = 1.0 / D
    neg = (idx[:, None] >= idx[None, :]).astype(np.float32) * BIG
    c[:, C_NEGP * 128:(C_NEGP + 1) * 128] = neg
    c[:, C_NEGN * 128:(C_NEGN + 1) * 128] = -neg
    c[:, C_ONE * 128:(C_ONE + 1) * 128] = 1.0
    rc = np.zeros((128, 64), np.float32)
    for g, w in enumerate(POOL_W):
        t = np.arange(16)
        rc[:, g * 16:(g + 1) * 16] = 1.0 / np.minimum(t + 1, w).astype(np.float32)
    import ml_dtypes
    return c.astype(ml_dtypes.bfloat16), rc


_CONSTS = make_consts()


def prep_inputs(x, p, a_norm, a_w_in, a_w_group, a_scale, a_w_out, kv_norm, w_kv, k_norm,
                b_norm, b_w_in, b_q_norm, b_w_out, ple_w, ple_gate_w):
    f = lambda a: np.ascontiguousarray(np.asarray(a, dtype=np.float32))
    x = f(x); p = f(p)
    B = x.shape[0]
    vecs = np.zeros((128, NVEC), np.float32)
    col = lambda v: f(v).reshape(NC_, 128).T
    vecs[:, V_ANORM:V_ANORM + 8] = col(a_norm[0])
    vecs[:, V_ASCALE:V_ASCALE + 8] = col(a_scale[0])
    vecs[:, V_KVN:V_KVN + 8] = col(kv_norm)
    vecs[:, V_BN:V_BN + 8] = col(b_norm[0])
    vecs[:, V_KN] = np.tile(f(k_norm), 2)
    vecs[:, V_QN] = np.tile(f(b_q_norm[0]), 2)
    shared = {
        "vecs": vecs,
        "cst": _CONSTS[0],
        "rc": _CONSTS[1],
        "a_w_in": f(a_w_in[0]),
        "a_w_group": f(a_w_group[0]).reshape(D, 256),
        "a_w_out": f(a_w_out[0]),
        "w_kv_s": f(f(w_kv).reshape(D, 16, 128).transpose(1, 0, 2)),
        "b_w_in_s": f(f(b_w_in[0]).reshape(D, 16, 128).transpose(1, 0, 2)),
        "b_w_out": f(b_w_out[0]),
        "ple_w": f(ple_w),
        "ple_gate_w": f(ple_gate_w),
    }
    in_maps = []
    for b in range(B):
        m = dict(shared)
        m["xT"] = f(x[b].T)
        m["pT"] = f(p[:, b].transpose(0, 2, 1))
        in_maps.append(m)
    return in_maps


_CACHE = {}


def get_nc(n_layers=2):
    if n_layers not in _CACHE:
        nc = bass.Bass("TRN2", target_bir_lowering=False)
        build_program(nc, n_layers=n_layers)
        _CACHE[n_layers] = nc
    return _CACHE[n_layers]


def kernel(**inputs):
    in_maps = prep_inputs(**inputs)
    nc = get_nc(2)
    res = run_bass_kernel_spmd(nc, in_maps, core_ids=list(range(8)))
    out = np.stack([np.asarray(r["outT"], dtype=np.float32).T for r in res.results], axis=0)
    return np.ascontiguousarray(out)
```

```python
import numpy as np
import concourse.bass as bass
import concourse.mybir as mybir
from concourse.bass_utils import run_bass_kernel_spmd

F32 = mybir.dt.float32
BF16 = mybir.dt.bfloat16
AF = mybir.ActivationFunctionType
ALU = mybir.AluOpType

D = 1024
S = 2048
NC_ = 8
TT = 512
NTT = S // TT
PLE = 256
EPS = 1e-6
BIG = 30000.0
SB_SCALE = 64 ** -0.5
POOL_W = (2, 4, 8, 16)
DBG = {'nchunks': 8, 'ntiles': None, 'stage': 9, 'dummy': 1}

V_ANORM, V_ASCALE, V_KVN, V_BN, V_KN, V_QN = 0, 8, 16, 24, 32, 33
NVEC = 34
C_ID, C_TRI, C_BLK, C_OND, C_NEGP, C_NEGN, C_ONE = range(7)
NCB = 7


class Prog:
    CHUNK = 4000
    NDMA = 20

    def __init__(self, nc, sems):
        self.nc = nc
        self.engs = {'pe': nc.tensor, 'act': nc.scalar, 'dve': nc.vector,
                     'pool': nc.gpsimd, 'sp': nc.sync}
        self.free_sems = list(sems)
        self.esem = {e: [] for e in self.engs}
        self.ecount = {e: 0 for e in self.engs}
        self.dsem = [self.free_sems.pop() for _ in range(self.NDMA)]
        self.dcount = [0] * self.NDMA
        self.dlast = [None] * self.NDMA
        self.dnext = 0
        self.known = {e: {} for e in self.engs}
        self.last_w = {}
        self.readers = {}
        self.uid = 0
        self.ops = []
        self.events = {}
        self.phase_start = 0
        self.barrier_events = []
        self.n_inst = 0

    def op(self, eng, fn, reads=(), writes=(), dma=False):
        uid = self.uid
        self.uid += 1
        writes = tuple(writes) + tuple(k for k in reads if k.startswith('ps') and k not in writes)
        reads = tuple(k for k in reads if not k.startswith('ps'))
        deps = set()
        for k in reads:
            w = self.last_w.get(k)
            if w is not None:
                deps.add(w)
        for k in writes:
            w = self.last_w.get(k)
            if w is not None:
                deps.add(w)
            for r in self.readers.get(k, ()):
                deps.add(r)
        for k in reads:
            self.readers.setdefault(k, []).append(uid)
        for k in writes:
            self.last_w[k] = uid
            self.readers[k] = []
        deps.discard(uid)
        self.ops.append(dict(uid=uid, eng=eng, fn=fn, dma=dma, deps=deps))
        return uid

    def _wait(self, eng, ev):
        sem, val = ev
        kn = self.known[eng]
        if kn.get(id(sem), 0) >= val:
            return
        self.engs[eng].wait_ge(sem, val)
        kn[id(sem)] = val
        self.n_inst += 1

    def emit(self):
        ops = self.ops
        self.ops = []
        info = {o['uid']: o for o in ops}
        ps = self.phase_start
        signaling = set()
        for o in ops:
            best = {}
            dmas = []
            for d in o['deps']:
                if d < ps:
                    continue
                p = info[d]
                if p['dma']:
                    dmas.append(d)
                    continue
                if p['eng'] == 'pe' and o['eng'] == 'pe' and not o['dma']:
                    continue
                if d > best.get(p['eng'], -1):
                    best[p['eng']] = d
            o['wdeps'] = sorted(best.values()) + sorted(dmas)
            for d in best.values():
                signaling.add(d)
        last_on_eng = {}
        for o in ops:
            if not o['dma']:
                last_on_eng[o['eng']] = o['uid']
        for u in last_on_eng.values():
            signaling.add(u)
        first_seen = set()
        dma_events = []
        for o in ops:
            eng = o['eng']
            if eng not in first_seen:
                first_seen.add(eng)
                for ev in self.barrier_events:
                    self._wait(eng, ev)
            for d in o['wdeps']:
                self._wait(eng, self.events[d])
            if o['dma'] and eng == 'pool':
                sem = self.free_sems.pop()
                inst = o['fn'](self.engs[eng])
                inst.then_inc(sem, 16)
                ev = (sem, 16)
                self.events[o['uid']] = ev
                dma_events.append(ev)
            elif o['dma']:
                r = self.dnext
                self.dnext = (self.dnext + 1) % self.NDMA
                if self.dlast[r] is not None:
                    self._wait(eng, self.dlast[r])
                inst = o['fn'](self.engs[eng])
                self.dcount[r] += 16
                inst.then_inc(self.dsem[r], 16)
                ev = (self.dsem[r], self.dcount[r])
                self.dlast[r] = ev
                self.events[o['uid']] = ev
                dma_events.append(ev)
            else:
                inst = o['fn'](self.engs[eng])
                if o['uid'] in signaling:
                    c = self.ecount[eng]
                    k = c // self.CHUNK
                    while len(self.esem[eng]) <= k:
                        self.esem[eng].append(self.free_sems.pop())
                    sem = self.esem[eng][k]
                    inst.then_inc(sem, 1)
                    self.ecount[eng] = c + 1
                    self.events[o['uid']] = (sem, c - k * self.CHUNK + 1)
            self.n_inst += 1
        be = []
        for e, u in last_on_eng.items():
            be.append(self.events[u])
        seen = {}
        for ev in dma_events:
            seen[id(ev[0])] = ev
        be.extend(seen.values())
        self.barrier_events = be + [ev for ev in self.barrier_events
                                    if all(id(ev[0]) != id(b[0]) for b in be)]
        self.phase_start = self.uid

    def final_wait(self, eng='sp'):
        for ev in self.barrier_events:
            self._wait(eng, ev)


def I(name, *a, **k):
    return lambda e: getattr(e, name)(*a, **k)


def SEQ(*fs):
    def fn(e):
        inst = None
        for f in fs:
            inst = f(e)
        return inst
    return fn


def build_program(nc, n_layers=2):
    from contextlib import ExitStack

    def din(name, shape):
        return nc.dram_tensor(name, list(shape), F32, kind="ExternalInput").ap()

    xT_d = din("xT", [D, S])
    pT_d = din("pT", [2, PLE, S])
    vecs_d = din("vecs", [128, NVEC])
    cst_d = nc.dram_tensor("cst", [128, NCB * 128], BF16, kind="ExternalInput").ap()
    rc_d = din("rc", [128, 64])
    awin_d = din("a_w_in", [D, 2 * D])
    awg_d = din("a_w_group", [D, 256])
    awout_d = din("a_w_out", [D, D])
    wkv_d = din("w_kv_s", [16, D, 128])
    bwin_d = din("b_w_in_s", [16, D, 128])
    bwout_d = din("b_w_out", [D, D])
    plew_d = din("ple_w", [2, PLE, D])
    gate_d = din("ple_gate_w", [2, D, D])
    out_d = nc.dram_tensor("outT", [D, S], F32, kind="ExternalOutput").ap()
    dbg_d = nc.dram_tensor("dbgy", [D, S], F32, kind="ExternalOutput").ap() if DBG.get('dump') else None

    es = ExitStack()

    def sb(name, shape, dt):
        return es.enter_context(nc.sbuf_tensor(name, list(shape), dt))

    sems = [es.enter_context(nc.semaphore(f"s{i}")) for i in range(100)]
    P = Prog(nc, sems)

    xT = sb("xT_sb", [128, NC_, S], F32)
    vecs = sb("vecs_sb", [128, NVEC], F32)
    rcf = sb("rcf", [128, 64], F32)
    cstb = sb("cstb", [128, NCB * 128], BF16)
    zeros64 = sb("zeros64", [128, 64], BF16)
    qgain = sb("qgain", [128, 1], F32)
    sqb = [sb(f"sqb{i}", [128, TT], BF16) for i in range(2)]
    lnv = [sb(f"lnv{i}", [128, TT], F32) for i in range(2)]
    rstd = lnv
    ps = es.enter_context(nc.psum_tensor("ps", [128, 8, TT], F32))

    def cb(i):
        return cstb[:, i * 128:(i + 1) * 128]

    ident, tri, blk, ond, negp, negn, onef = [cb(i) for i in range(NCB)]
    rc = rcf[:, :]

    st_i = [0]
    stage = []

    def next_stage():
        r = st_i[0] % len(stage)
        st_i[0] += 1
        return r

    bank_i = [0]

    def next_bank():
        b = bank_i[0]
        bank_i[0] = (b + 1) % 8
        return b

    def xk(c, tt):
        return f'x{c}.{tt}'

    def xap(c, tt):
        return xT[:, c, tt * TT:(tt + 1) * TT]

    P.op('sp', I('dma_start', out=vecs[:], in_=vecs_d), writes=['vecs'], dma=True)
    P.op('sp', I('dma_start', out=cstb[:], in_=cst_d), writes=['cstb'], dma=True)
    P.op('sp', I('dma_start', out=rcf[:], in_=rc_d), writes=['cstf'], dma=True)
    P.op('pool', I('memset', zeros64[:], 0.0), writes=['zeros64'])
    P.op('pool', I('tensor_scalar', qgain[:], vecs[:, V_QN:V_QN + 1], SB_SCALE, 1.0, ALU.mult, ALU.mult),
         reads=['vecs'], writes=['qgain'])
    def load_x_tile(tt):
        for c in range(NC_):
            P.op('sp', I('dma_start', out=xap(c, tt), in_=xT_d[c * 128:(c + 1) * 128, tt * TT:(tt + 1) * TT]),
                 writes=[xk(c, tt)], dma=True)

    def load_w(dst, dkey, src2d, KC, N, gain_col0=None):
        W = stage[0].shape[-1]
        for kc in range(KC):
            for n0 in range(0, N, W):
                n1 = min(N, n0 + W)
                r = next_stage()
                P.op('sp', I('dma_start', out=stage[r][:, 0:n1 - n0], in_=src2d[kc * 128:(kc + 1) * 128, n0:n1]),
                     writes=[f'stage{r}'], dma=True)
                if gain_col0 is None:
                    P.op('pool', I('tensor_copy', dst[:, kc, n0:n1], stage[r][:, 0:n1 - n0]),
                         reads=[f'stage{r}'], writes=[f'{dkey}.{kc}'])
                else:
                    P.op('pool', I('tensor_scalar', dst[:, kc, n0:n1], stage[r][:, 0:n1 - n0],
                                   vecs[:, gain_col0 + kc:gain_col0 + kc + 1], 1.0, ALU.mult, ALU.mult),
                         reads=[f'stage{r}', 'vecs'], writes=[f'{dkey}.{kc}'])

    def mm_group(bank_ap, pairs, bkey, reads):
        n = len(pairs)
        fs = [I('matmul', bank_ap, l, r, start=(i == 0), stop=(i == n - 1)) for i, (l, r) in enumerate(pairs)]
        P.op('pe', SEQ(*fs), reads=reads, writes=[bkey])

    def rms_rstd(tt, slot):
        b = next_bank()
        for c in range(NC_):
            j = c % 2
            P.op('act', I('activation', out=sqb[j][:], in_=xap(c, tt), func=AF.Square),
                 reads=[xk(c, tt)], writes=[f'sqb{j}'])
            P.op('pe', I('matmul', ps[:, b, :], ond, sqb[j][:], start=(c == 0), stop=(c == NC_ - 1)),
                 reads=[f'sqb{j}', 'cstb'], writes=[f'ps{b}'])
        P.op('act', I('activation', out=lnv[slot][:], in_=ps[:, b, :], func=AF.Ln, bias=EPS),
             reads=[f'ps{b}'], writes=[f'lnv{slot}'])
        P.op('act', I('activation', out=rstd[slot][:], in_=lnv[slot][:], func=AF.Exp, scale=-0.5),
             reads=[f'lnv{slot}'], writes=[f'lnv{slot}'])

    def ple_phase(layer, Wple, Wgate, B):
        seen = set()

        def wk(k):
            if k in seen:
                return [k]
            seen.add(k)
            return [k] + B.get('alias', {}).get(k, [])
        def bufs(tt):
            alt = (tt % 2 == 1) and ('x1b2' in B)
            return ((B['x1b2'], B['x1b2k'], B['pb2'], B['pb2k']) if alt else (B['x1b'], B['x1bk'], B['pb'], B['pbk']))

        def prep(tt):
            x1b, x1bk, pbs, pbks = bufs(tt)
            t0 = tt * TT
            for kc in range(2):
                P.op('pool', I('dma_start', out=pbs[kc], in_=pT_d[layer, kc * 128:(kc + 1) * 128, t0:t0 + TT]),
                     writes=wk(pbks[kc]), dma=True)
            for c in range(NC_):
                P.op('act', I('activation', out=x1b[c], in_=xap(c, tt), func=AF.Copy),
                     reads=[xk(c, tt)], writes=wk(x1bk[c]))

        prep(0)
        for tt in range(NTT):
            t0 = tt * TT
            x1b, x1bk, pbs, pbks = bufs(tt)
            if tt + 1 < NTT and 'x1b2' in B:
                prep(tt + 1)
            for n in range(NC_):
                ba = next_bank()
                bg = next_bank()
                mm_group(ps[:, ba, :], [(Wple[:, kc, n * 128:(n + 1) * 128], pbs[kc]) for kc in range(2)],
                         f'ps{ba}', B['wplek'] + pbks)
                mm_group(ps[:, bg, :], [(Wgate[:, kc, n * 128:(n + 1) * 128], x1b[kc]) for kc in range(NC_)],
                         f'ps{bg}', B['wgatek'] + x1bk)
                j = n % 2
                P.op('act', I('activation', out=B['sg'][j], in_=ps[:, bg, :], func=AF.Sigmoid),
                     reads=[f'ps{bg}'], writes=wk(B['sgk'][j]))
                P.op('dve', I('tensor_tensor', B['tm'][j], ps[:, ba, :], B['sg'][j], ALU.mult),
                     reads=[f'ps{ba}', B['sgk'][j]], writes=wk(B['tmk'][j]))
                P.op('pool', I('tensor_tensor', xap(n, tt), xap(n, tt), B['tm'][j], ALU.add),
                     reads=[B['tmk'][j], xk(n, tt)], writes=[xk(n, tt)])
                if B.get('store'):
                    P.op('sp', I('dma_start', out=out_d[n * 128:(n + 1) * 128, t0:t0 + TT], in_=xap(n, tt)),
                         reads=[xk(n, tt)], writes=[f'out.{n}.{tt}'], dma=True)
            if tt + 1 < NTT and 'x1b2' not in B:
                prep(tt + 1)

    def outproj_phase(Wout, y_ap_fn, ykeys_fn, tts):
        for tt in tts:
            for n in range(NC_):
                b = next_bank()
                mm_group(ps[:, b, :], [(Wout[:, dc, n * 128:(n + 1) * 128], y_ap_fn(dc, tt)) for dc in range(NC_)],
                         f'ps{b}', [f'wout.{k}' for k in range(NC_)] + [ykeys_fn(k, tt) for k in range(NC_)])
                P.op('dve', I('tensor_tensor', xap(n, tt), ps[:, b, :], xap(n, tt), ALU.add),
                     reads=[f'ps{b}', xk(n, tt)], writes=[xk(n, tt)])

    with ExitStack() as l0:
        def sb0(name, shape, dt):
            return l0.enter_context(nc.sbuf_tensor(name, list(shape), dt))
        Win = sb0("Win", [128, NC_, 2 * D], BF16)
        Wg = sb0("Wg", [128, 8, 256], BF16)
        Wout = sb0("Wout0", [128, NC_, D], BF16)
        xh = sb0("xh", [128, NC_, TT], BF16)
        ubuf = sb0("ubuf", [128, NC_, 16 + TT], F32)
        sAB = [sb0(f"sAB{i}", [128, 16 + TT], F32) for i in range(4)]
        pooled = [sb0(f"pooled{i}", [128, 2, TT], BF16) for i in range(2)]
        sz = [sb0(f"sz{i}", [128, 2, TT], BF16) for i in range(2)]
        y2 = [sb0(f"y0_{i}", [128, NC_, TT], BF16) for i in range(2)]
        fixt = sb0("fixt", [128, 16], F32)

        Wple0 = sb0("Wple0", [128, 2, D], BF16)
        Wgate0 = sb0("Wgate0", [128, NC_, D], BF16)

        def cast_dma(dst, dkeys, srcv):
            P.op('pool', I('dma_start', out=dst, in_=srcv), writes=dkeys, dma=True)

        def load_win_group(g):
            for h in range(2):
                c0 = h * D + g * 256
                srcv = awin_d.rearrange("(kc p) n -> p kc n", p=128)[:, :, c0:c0 + 256]
                cast_dma(Win[:, :, c0:c0 + 256], [f'win.g{g}'], srcv)

        P.op('pool', I('memset', ubuf[:], 0.0), writes=[f'ubuf{c}' for c in range(NC_)])
        load_x_tile(0)
        load_win_group(0)
        load_win_group(1)
        load_x_tile(1)
        load_win_group(2)
        load_win_group(3)
        cast_dma(Wg[:], [f'wg.{k}' for k in range(8)], awg_d.rearrange("(kc p) n -> p kc n", p=128))
        cast_dma(Wout[:], [f'wout.{k}' for k in range(NC_)], awout_d.rearrange("(kc p) n -> p kc n", p=128))
        load_x_tile(2)
        load_x_tile(3)
        cast_dma(Wple0[:], ['wple.0', 'wple.1'], plew_d[0].rearrange("(kc p) n -> p kc n", p=128))
        cast_dma(Wgate0[:], [f'wgate.{k}' for k in range(NC_)], gate_d[0].rearrange("(kc p) n -> p kc n", p=128))

        L = 16 + TT
        xh_keys = [f'xh.{k}' for k in range(NC_)]

        def xhmul(tt):
            slot = tt % 2
            for c in range(NC_):
                P.op('dve', I('scalar_tensor_tensor', xh[:, c, :], xap(c, tt), vecs[:, V_ANORM + c:V_ANORM + c + 1],
                              rstd[slot][:], ALU.mult, ALU.mult),
                     reads=[xk(c, tt), f'lnv{slot}', 'vecs'], writes=[f'xh.{c}'])

        def outproj_chunks(tt, ns):
            yb = tt % 2
            for n in ns:
                b = next_bank()
                mm_group(ps[:, b, :], [(Wout[:, dc, n * 128:(n + 1) * 128], y2[yb][:, dc, :]) for dc in range(NC_)],
                         f'ps{b}', [f'wout.{k}' for k in range(NC_)] + [f'y{yb}.{k}' for k in range(NC_)])
                P.op('dve', I('tensor_tensor', xap(n, tt), ps[:, b, :], xap(n, tt), ALU.add),
                     reads=[f'ps{b}', xk(n, tt)], writes=[xk(n, tt)])

        osched = {0: [2, 3], 1: [4, 5], 2: [6, 7], 3: []}
        rms_rstd(0, 0)
        xhmul(0)
        for tt in range(NTT):
            y = y2[tt % 2]
            ykey = f'y{tt % 2}'
            def front(g):
                w = POOL_W[g]
                pj = g % 2
                for cl in range(2):
                    c = 2 * g + cl
                    bu = next_bank()
                    mm_group(ps[:, bu, :], [(Win[:, kc, c * 128:(c + 1) * 128], xh[:, kc, :]) for kc in range(NC_)],
                             f'ps{bu}', xh_keys + [f'win.g{g}'])
                    P.op('act', I('activation', out=ubuf[:, c, 16:16 + TT], in_=ps[:, bu, :], func=AF.Copy),
                         reads=[f'ps{bu}'], writes=[f'ubuf{c}'])
                    bz = next_bank()
                    mm_group(ps[:, bz, :], [(Win[:, kc, D + c * 128:D + (c + 1) * 128], xh[:, kc, :]) for kc in range(NC_)],
                             f'ps{bz}', xh_keys + [f'win.g{g}'])
                    P.op('act', I('activation', out=sz[pj][:, cl, :], in_=ps[:, bz, :], func=AF.Silu),
                         reads=[f'ps{bz}'], writes=[f'sz{pj}.{cl}'])
                    src = ubuf[:, c, :]
                    skey = f'ubuf{c}'
                    bufs = [sAB[2 * cl], sAB[2 * cl + 1]]
                    bkeys = [f'sAB{2 * cl}', f'sAB{2 * cl + 1}']
                    step, lo, bi = 1, 0, 0
                    while step < w:
                        lo2 = lo + step
                        dst = bufs[bi]
                        P.op('dve', I('tensor_tensor', dst[:, lo2:L], src[:, lo2:L], src[:, lo2 - step:L - step], ALU.add),
                             reads=[skey], writes=[bkeys[bi]])
                        src = dst[:, :]
                        skey = bkeys[bi]
                        bi ^= 1
                        lo = lo2
                        step *= 2
                    P.op('dve', I('scalar_tensor_tensor', pooled[pj][:, cl, :], src[:, 16:16 + TT], 1.0 / w,
                                  ubuf[:, c, 16:16 + TT], ALU.mult, ALU.subtract),
                         reads=[skey, f'ubuf{c}'], writes=[f'pooled{pj}.{cl}'])
                    if tt == 0:
                        P.op('dve', I('tensor_tensor', fixt[:, 0:w - 1], src[:, 16:16 + w - 1],
                                      rc[:, g * 16:g * 16 + w - 1], ALU.mult),
                             reads=[skey, 'cstf'], writes=['fixt'])
                        P.op('dve', I('tensor_tensor', pooled[pj][:, cl, 0:w - 1], fixt[:, 0:w - 1],
                                      ubuf[:, c, 16:16 + w - 1], ALU.subtract),
                             reads=['fixt', f'ubuf{c}'], writes=[f'pooled{pj}.{cl}'])
                    P.op('pool', I('tensor_copy', ubuf[:, c, 0:16], ubuf[:, c, TT:TT + 16]),
                         reads=[f'ubuf{c}'], writes=[f'ubuf{c}'])

            def back(g):
                pj = g % 2
                if tt > 0:
                    outproj_chunks(tt - 1, osched[g])
                for dl in range(2):
                    dc = 2 * g + dl
                    bm = next_bank()
                    mm_group(ps[:, bm, :], [(Wg[:, 2 * g + kc, dl * 128:(dl + 1) * 128], pooled[pj][:, kc, :]) for kc in range(2)],
                             f'ps{bm}', [f'wg.{2 * g}', f'wg.{2 * g + 1}', f'pooled{pj}.0', f'pooled{pj}.1'])
                    P.op('dve', I('scalar_tensor_tensor', y[:, dc, :], ps[:, bm, :],
                                  vecs[:, V_ASCALE + dc:V_ASCALE + dc + 1], sz[pj][:, dl, :], ALU.mult, ALU.mult),
                         reads=[f'ps{bm}', 'vecs', f'sz{pj}.{dl}'], writes=[f'{ykey}.{dc}'])
                if g == 1 and tt + 1 < NTT:
                    rms_rstd(tt + 1, (tt + 1) % 2)

            front(0)
            for g in range(4):
                if g + 1 < 4:
                    front(g + 1)
                if g == 2 and tt + 1 < NTT:
                    xhmul(tt + 1)
                back(g)
            outproj_chunks(tt, [0, 1])
        outproj_chunks(NTT - 1, [2, 3, 4, 5, 6, 7])
        B0 = dict(x1b=[xh[:, c, :] for c in range(NC_)], x1bk=[f'xh.{c}' for c in range(NC_)],
                  pst=[ubuf[:, k, 16:16 + TT] for k in range(2)], pstk=['ubuf0', 'ubuf1'],
                  pb=[pooled[0][:, k, :] for k in range(2)], pbk=['pooled0.0', 'pooled0.1'],
                  x1b2=[y2[0][:, c, :] for c in range(NC_)], x1b2k=[f'y0.{c}' for c in range(NC_)],
                  pb2=[pooled[1][:, k, :] for k in range(2)], pb2k=['pooled1.0', 'pooled1.1'],
                  sg=[sAB[0][:, 0:TT], sAB[1][:, 0:TT]], sgk=['sAB0', 'sAB1'],
                  tm=[sAB[2][:, 0:TT], sAB[3][:, 0:TT]], tmk=['sAB2', 'sAB3'],
                  wplek=['wple.0', 'wple.1'], wgatek=[f'wgate.{k}' for k in range(NC_)])
        ple_phase(0, Wple0, Wgate0, B0)
        P.emit()

    def ple_scope(layer):
        with ExitStack() as l:
            def sbl(name, shape, dt):
                return l.enter_context(nc.sbuf_tensor(name, list(shape), dt))
            stage[:] = [sbl(f"stagep{layer}_{i}", [128, D], F32) for i in range(2)]
            Wple = sbl(f"Wple{layer}", [128, 2, D], BF16)
            Wgate = sbl(f"Wgate{layer}", [128, NC_, D], BF16)
            x1b = sbl(f"x1b{layer}", [128, NC_, TT], BF16)
            pstage = sbl(f"pstage{layer}", [128, 2, TT], F32)
            pb = sbl(f"pb{layer}", [128, 2, TT], BF16)
            sg = [sbl(f"sg{layer}{i}", [128, TT], F32) for i in range(2)]
            tm = [sbl(f"tm{layer}{i}", [128, TT], F32) for i in range(2)]
            load_w(Wple, 'wple', plew_d[layer], 2, D)
            load_w(Wgate, 'wgate', gate_d[layer], NC_, D)
            B = dict(x1b=[x1b[:, c, :] for c in range(NC_)], x1bk=[f'x1b.{c}' for c in range(NC_)],
                     pst=[pstage[:, k, :] for k in range(2)], pstk=['pstage.0', 'pstage.1'],
                     pb=[pb[:, k, :] for k in range(2)], pbk=['pb.0', 'pb.1'],
                     sg=[sg[0][:], sg[1][:]], sgk=['sg0', 'sg1'], tm=[tm[0][:], tm[1][:]], tmk=['tm0', 'tm1'],
                     wplek=['wple.0', 'wple.1'], wgatek=[f'wgate.{k}' for k in range(NC_)])
            ple_phase(layer, Wple, Wgate, B)
            P.emit()


    if n_layers >= 2:
        with ExitStack() as l1:
            y1 = l1.enter_context(nc.sbuf_tensor("y1", [128, NC_, S], BF16))
            with ExitStack() as l1a:
                def sba(name, shape, dt):
                    return l1a.enter_context(nc.sbuf_tensor(name, list(shape), dt))
                xh1 = sba("xh1", [128, NC_, S], BF16)
                KT2 = [sba(f"KT{i}", [128, S], BF16) for i in range(2)]
                QT2 = [sba(f"QT{i}", [128, S], BF16) for i in range(2)]
                Vt2 = [sba(f"Vt{i}", [128, 16, 128], BF16) for i in range(2)]
                szc2 = [sba("szc0", [128, S], BF16)] * 2
                wsl = {k: sba(f"wsl_{k}", [128, NC_, 128], BF16) for k in ('k', 'v', 'q', 'z')}
                stg1 = [sba("stg1_0", [128, NC_, 128], F32)] * 2
                zeros128 = sba("zeros128", [128, 128], BF16)
                ebuf = [sba(f"ebuf{i}", [128, 2, TT], F32) for i in range(3)]
                spb = [sba(f"spb{i}", [128, 2, TT], BF16) for i in range(2)]
                gbuf = [sba(f"gbuf{i}", [128, 2, TT], F32) for i in range(2)]
                abuf = [sba("abuf0", [128, 2, TT], BF16)] * 2
                rrow2 = sba("rrow2", [64, TT], BF16)
                LB, OB = 0, 4
                spare = [5, 6]
                sp_i = [0]

                owned = set()

                def spare_bank():
                    b = spare[sp_i[0] % len(spare)]
                    sp_i[0] += 1
                    assert b not in owned, f"spare bank {b} still owned"
                    owned.add(b)
                    return b

                def release(b):
                    owned.discard(b)

                P.op('pool', I('memset', zeros128[:], 0.0), writes=['zeros128'])

                st1_i = [0]

                def proj_items(cc, which):
                    bs = cc % 2
                    KT, QT, Vt, szc = KT2[bs], QT2[bs], Vt2[bs], szc2[bs]
                    items = []

                    def w_item(nm, src, col0):
                        def st0():
                            P.op('sp', I('dma_start', out=stg1[0][:], in_=src.rearrange("(kc p) n -> p kc n", p=128)),
                                 writes=['stg1'], dma=True)

                        def cast(kcs):
                            def f():
                                for kc in kcs:
                                    P.op('pool', I('tensor_scalar', wsl[nm][:, kc, :], stg1[0][:, kc, :],
                                                   vecs[:, col0 + kc:col0 + kc + 1], 1.0, ALU.mult, ALU.mult),
                                         reads=['stg1', 'vecs'], writes=[f'wsl{nm}'])
                            return f
                        if 'W' in which:
                            return [st0, lambda: None, cast([0, 1, 2, 3]), cast([4, 5, 6, 7])]
                        return [lambda: (st0(), cast(range(NC_))())]

                    def kq_item(nm, dst, dkey, gain_ap, tt):
                        st = {}

                        def st0():
                            st['b'] = spare_bank()
                            b = st['b']
                            mm_group(ps[:, b, :], [(wsl[nm][:, kc, :], xh1[:, kc, tt * TT:(tt + 1) * TT]) for kc in range(NC_)],
                                     f'ps{b}', [f'wsl{nm}'] + [f'xh1.{k}.{tt}' for k in range(NC_)])

                        def st1():
                            b = st['b']
                            j = tt % 2
                            st['j'] = j
                            P.op('act', I('activation', out=sqb[j][:], in_=ps[:, b, :], func=AF.Square),
                                 reads=[f'ps{b}'], writes=[f'sqb{j}'])
                            st['b2'] = spare_bank()
                            b2 = st['b2']
                            P.op('pe', I('matmul', ps[:, b2, :], blk, sqb[j][:], start=True, stop=True),
                                 reads=[f'sqb{j}', 'cstb'], writes=[f'ps{b2}'])

                        def st2():
                            b, b2, j = st['b'], st['b2'], st['j']
                            P.op('act', I('activation', out=lnv[j][:], in_=ps[:, b2, :], func=AF.Ln, bias=EPS),
                                 reads=[f'ps{b2}'], writes=[f'lnv{j}'])
                            P.op('act', I('activation', out=rstd[j][:], in_=lnv[j][:], func=AF.Exp, scale=-0.5),
                                 reads=[f'lnv{j}'], writes=[f'lnv{j}'])
                            P.op('dve', I('scalar_tensor_tensor', dst[:, tt * TT:(tt + 1) * TT], ps[:, b, :], gain_ap,
                                          rstd[j][:], ALU.mult, ALU.mult),
                                 reads=[f'ps{b}', f'lnv{j}', 'vecs', 'qgain'], writes=[f'{dkey}{bs}.{tt}'])
                            release(b)
                            release(b2)
                        return [st0, st1, st2]

                    def v_item(bk):
                        st = {}

                        def st0():
                            st['b'] = spare_bank()
                            b = st['b']
                            mm_group(ps[:, b, 0:128], [(xh1[:, kc, bk * 128:(bk + 1) * 128], wsl['v'][:, kc, :]) for kc in range(NC_)],
                                     f'ps{b}', ['wslv'] + [f'xh1.{k}.{bk // 4}' for k in range(NC_)])

                        def st1():
                            b = st['b']
                            P.op('dve', I('tensor_copy', Vt[:, bk, :], ps[:, b, 0:128]),
                                 reads=[f'ps{b}'], writes=[f'Vt{bs}.{bk}'])
                            release(b)
                        return [st0, st1]

                    def z_item(tt):
                        st = {}

                        def st0():
                            st['b'] = spare_bank()
                            b = st['b']
                            mm_group(ps[:, b, :], [(wsl['z'][:, kc, :], xh1[:, kc, tt * TT:(tt + 1) * TT]) for kc in range(NC_)],
                                     f'ps{b}', ['wslz'] + [f'xh1.{k}.{tt}' for k in range(NC_)])

                        def st1():
                            b = st['b']
                            P.op('act', I('activation', out=lnv[0][:], in_=ps[:, b, :], func=AF.Exp, scale=-1.0),
                                 reads=[f'ps{b}'], writes=['lnv0'])
                            P.op('act', I('activation', out=lnv[0][:], in_=lnv[0][:], func=AF.Ln, bias=1.0),
                                 reads=['lnv0'], writes=['lnv0'])
                            P.op('act', I('activation', out=lnv[0][:], in_=lnv[0][:], func=AF.Exp, scale=-1.0),
                                 reads=['lnv0'], writes=['lnv0'])
                            P.op('dve', I('tensor_tensor', szc[:, tt * TT:(tt + 1) * TT], ps[:, b, :], lnv[0][:], ALU.mult),
                                 reads=['lnv0', f'ps{b}'], writes=[f'szc.{tt}'])
                            release(b)
                        return [st0, st1]

                    if 'w' in which or 'W' in which:
                        for wi in (w_item('k', wkv_d[cc], V_KVN), w_item('q', bwin_d[cc], V_BN),
                                   w_item('v', wkv_d[8 + cc], V_KVN), w_item('z', bwin_d[8 + cc], V_BN)):
                            items.append(wi)
                            if 'W' in which:
                                items.extend([[] for _ in range(3)])
                    kq = []
                    if 'k' in which or 'K' in which:
                        for tt in range(NTT):
                            kq.append(kq_item('k', KT, 'KT', vecs[:, V_KN:V_KN + 1], tt))
                    if 'k' in which or 'Q' in which:
                        for tt in range(NTT):
                            kq.append(kq_item('q', QT, 'QT', qgain[:], tt))
                    if 'K' in which or 'Q' in which:
                        return kq
                    vz = []
                    if 'v' in which:
                        for bk in range(16):
                            vz.append(v_item(bk))
                    if 'z' in which:
                        for tt in range(NTT):
                            vz.append(z_item(tt))
                    if which in ('v', 'z'):
                        return vz
                    while kq or vz:
                        if kq:
                            items.append(kq.pop(0))
                        for _ in range(2):
                            if vz:
                                items.append(vz.pop(0))
                    return items

                def run_stages(items, i, order=(5, 4, 3, 2, 1, 0)):
                    for s in order:
                        k = i - s
                        if 0 <= k < len(items) and s < len(items[k]):
                            items[k][s]()

                spare[:] = [5, 6, 7, 0, 1, 2, 3]
                itw = proj_items(0, 'w')
                for i in range(len(itw) + 6):
                    run_stages(itw, i)
                for tt in range(NTT):
                    slot = tt % 2
                    rms_rstd(tt, slot)
                    for c in range(NC_):
                        P.op('dve', I('tensor_tensor', xh1[:, c, tt * TT:(tt + 1) * TT], xap(c, tt), rstd[slot][:], ALU.mult),
                             reads=[xk(c, tt), f'lnv{slot}'], writes=[f'xh1.{c}.{tt}'])
                it0 = proj_items(0, 'vkz')
                for i in range(len(it0) + 6):
                    run_stages(it0, i, order=(0, 1, 2, 3, 4, 5))

                Wout1v = xh1[:, 0:4, :].rearrange("p a (b n) -> p (a b) n", b=2)
                Wgate1v = xh1[:, 4:8, :].rearrange("p a (b n) -> p (a b) n", b=2)
                Wple1v = KT2[0][:, :].rearrange("p (a n) -> p a n", a=2)
                stgA = stg1[0][:].rearrange("p kc n -> p (kc n)")
                stgB = QT2[0].bitcast(F32)
                xh1_lo = [f'xh1.{cc}.{t}' for cc in range(0, 4) for t in range(NTT)]
                xh1_hi = [f'xh1.{cc}.{t}' for cc in range(4, 8) for t in range(NTT)]
                kt0_keys = [f'KT0.{t}' for t in range(NTT)]
                qt0_keys = [f'QT0.{t}' for t in range(NTT)]

                def end_items():
                    def st0():
                        P.op('pool', I('dma_start', out=Wout1v, in_=bwout_d.rearrange("(kc p) n -> p kc n", p=128)),
                             writes=[f'wout.{k}' for k in range(NC_)] + xh1_lo, dma=True)
                        P.op('pool', I('dma_start', out=Wgate1v, in_=gate_d[1].rearrange("(kc p) n -> p kc n", p=128)),
                             writes=[f'wgate.{k}' for k in range(NC_)] + xh1_hi, dma=True)
                        P.op('pool', I('dma_start', out=Wple1v, in_=plew_d[1].rearrange("(kc p) n -> p kc n", p=128)),
                             writes=['wple.0', 'wple.1'] + kt0_keys, dma=True)
                    return [[], [], [st0]]

                spare[:] = [7]
                sp_i[0] = 0
                def outproj_item(n, tt, dcs):
                    st = {}

                    def st0():
                        st['b'] = spare_bank()
                        b = st['b']
                        mm_group(ps[:, b, :], [(Wout1v[:, dc, n * 128:(n + 1) * 128], y1[:, dc, tt * TT:(tt + 1) * TT]) for dc in dcs],
                                 f'ps{b}', [f'wout.{k}' for k in dcs] + [f'y1.{k}.{tt}' for k in dcs])

                    def st1():
                        b = st['b']
                        P.op('dve', I('tensor_tensor', xap(n, tt), ps[:, b, :], xap(n, tt), ALU.add),
                             reads=[f'ps{b}', xk(n, tt)], writes=[xk(n, tt)])
                        release(b)
                    return [st0, st1]

                tile_n = 0
                for c in range(DBG['nchunks']):
                    bs = c % 2
                    KT, QT, Vt, szc = KT2[bs], QT2[bs], Vt2[bs], szc2[bs]
                    if c + 1 < DBG['nchunks']:
                        nxt = proj_items(c + 1, 'W')
                        kk = proj_items(c + 1, 'K')
                        qq = proj_items(c + 1, 'Q')
                        vv = proj_items(c + 1, 'v')
                        zz = proj_items(c + 1, 'z')
                        nxtP = [[] for _ in range(41)]
                        for t_, it in enumerate(vv):
                            nxtP[14 + t_] = it
                        for t_, it in enumerate(zz[:3]):
                            nxtP[30 + t_] = it
                        nxtP[40] = zz[3]
                        for t_, it in enumerate(kk):
                            nxtP[4 + t_] = it
                        for t_, it in enumerate(qq):
                            nxtP[8 + t_] = it
                        kq_next = []
                    else:
                        nxt = end_items()
                        kq_next = []
                        nxtP = [[] for _ in range(10)]
                        for tt_ in range(NTT):
                            for n_ in range(NC_):
                                nxtP.append(outproj_item(n_, tt_, list(range(NC_ if tt_ < NTT - 1 else NC_ - 1))))
                    seq = []
                    for qt in range(NTT):
                        for kb in range(4 * qt + 3, -1, -1):
                            seq.append((qt, kb))
                    if DBG['ntiles'] is not None:
                        seq = seq[:DBG['ntiles']]
                    NS = len(seq)

                    KQW = 14

                    def cb_of(i):
                        if i < KQW:
                            return 2
                        return 5 if (i - KQW) % 2 == 0 else 2

                    def tinfo(i):
                        qt, kb = seq[i]
                        j = kb - 4 * qt
                        lo = 128 * j if j > 0 else 0
                        first = (kb == 4 * qt + 3)
                        if first:
                            lor = None
                        else:
                            jn = j + 1
                            lor = 128 * jn + 1 if jn >= 0 else 0
                        return qt, kb, j, lo, first, lor, tile_n + i

                    def rec_qk(i):
                        qt, kb, j, lo, first, lor, n = tinfo(i)
                        t0 = qt * TT
                        fs = []
                        for h in range(2):
                            p0 = 64 * h
                            fs.append(I('matmul', ps[:, LB + h, lo:TT], KT[p0:p0 + 64, kb * 128:(kb + 1) * 128],
                                        QT[p0:p0 + 64, t0 + lo:t0 + TT], start=True, stop=(j < 0)))
                        if j >= 0:
                            for h in range(2):
                                fs.append(I('matmul', ps[:, LB + h, lo:lo + 128], ident, negn, start=False, stop=True))
                        P.op('pe', SEQ(*fs), reads=[f'KT{bs}.{kb // 4}', f'QT{bs}.{qt}', 'cstb'],
                             writes=[f'ps{LB}', f'ps{LB + 1}'])

                    def rec_p1(i):
                        qt, kb, j, lo, first, lor, n = tinfo(i)
                        eb = n % 3
                        P.op('act', I('activation', out=ebuf[eb][:, :, lo:TT], in_=ps[:, LB:LB + 2, lo:TT], func=AF.Exp),
                             reads=[f'ps{LB}', f'ps{LB + 1}'], writes=[f'ebuf{eb}'])

                    def rec_p2(i):
                        qt, kb, j, lo, first, lor, n = tinfo(i)
                        eb = n % 3
                        sj = n % 2
                        P.op('act', I('activation', out=spb[sj][:, :, lo:TT], in_=ebuf[eb][:, :, lo:TT], func=AF.Ln, bias=1.0),
                             reads=[f'ebuf{eb}'], writes=[f'spb{sj}'])

                    def rec_tri(i):
                        qt, kb, j, lo, first, lor, n = tinfo(i)
                        CB = cb_of(i)
                        sj = n % 2
                        rj = n % 2
                        fs = []
                        for h in range(2):
                            fs.append(I('matmul', ps[:, CB + h, lo:TT], tri, spb[sj][:, h, lo:TT], start=True, stop=(lor is None)))
                        P.op('pe', SEQ(*fs), reads=[f'spb{sj}', 'cstb'], writes=[f'ps{CB}', f'ps{CB + 1}'])
                        if lor is not None:
                            fs = []
                            for h in range(2):
                                fs.append(I('matmul', ps[:, CB + h, lor:TT], onef[32 * h:32 * h + 1, :],
                                            rrow2[32 * h:32 * h + 1, lor:TT], start=False, stop=True))
                            P.op('pe', SEQ(*fs), reads=['rrow', 'cstb'], writes=[f'ps{CB}', f'ps{CB + 1}'])

                    def rec_rcopy(i):
                        qt, kb, j, lo, first, lor, n = tinfo(i)
                        CB = cb_of(i)
                        if kb > 0:
                            lorn = 128 * j + 1 if j >= 0 else 0
                            for h in range(2):
                                P.op('dve', I('tensor_copy', rrow2[32 * h:32 * h + 1, lorn:TT], ps[0:1, CB + h, lorn:TT]),
                                     reads=[f'ps{CB + h}'], writes=['rrow'])

                    def rec_p3(i):
                        qt, kb, j, lo, first, lor, n = tinfo(i)
                        CB = cb_of(i)
                        gj = n % 2
                        eb = n % 3
                        P.op('act', I('activation', out=gbuf[gj][:, :, lo:TT], in_=ps[:, CB:CB + 2, lo:TT], func=AF.Exp, scale=-1.0),
                             reads=[f'ps{CB}', f'ps{CB + 1}'], writes=[f'gbuf{gj}'])
                        rec_rcopy(i)
                        P.op('dve', I('tensor_tensor', abuf[0][:, 0, lo:TT], ebuf[eb][:, 0, lo:TT], gbuf[gj][:, 0, lo:TT], ALU.mult),
                             reads=[f'ebuf{eb}', f'gbuf{gj}'], writes=['abuf.0'])
                        P.op('pool', I('tensor_tensor', abuf[0][:, 1, lo:TT], ebuf[eb][:, 1, lo:TT], gbuf[gj][:, 1, lo:TT], ALU.mult),
                             reads=[f'ebuf{eb}', f'gbuf{gj}'], writes=['abuf.1'])

                    def rec_av(i):
                        qt, kb, j, lo, first, lor, n = tinfo(i)
                        aj = n % 2
                        fs = []
                        if first:
                            fs.append(I('matmul', ps[:, OB, :], zeros128[:], KT[:, 0:TT], start=True, stop=False))
                        for h in range(2):
                            p0 = 64 * h
                            fs.append(I('matmul', ps[p0:p0 + 64, OB, lo:TT], Vt[:, kb, p0:p0 + 64], abuf[aj][:, h, lo:TT],
                                        start=False, stop=(kb == 0)))
                        P.op('pe', SEQ(*fs), reads=['abuf.0', 'abuf.1', f'Vt{bs}.{kb}', 'zeros128', f'KT{bs}.0'],
                             writes=[f'ps{OB}'])
                        if kb == 0:
                            t0 = qt * TT
                            P.op('dve', I('tensor_tensor', y1[:, c, t0:t0 + TT], ps[:, OB, :], szc[:, t0:t0 + TT], ALU.mult),
                                 reads=[f'ps{OB}', f'szc.{qt}'], writes=[f'y1.{c}.{qt}'])

                    if NS > 0:
                        rec_qk(0)
                    nsteps = max(NS + 1 if NS > 0 else 0, len(nxt) + 6 if nxt else 0, len(nxtP) + 6 if nxtP else 0)
                    for i in range(nsteps):
                        if i < NS:
                            rec_p1(i)
                        if i + 1 < NS:
                            rec_qk(i + 1)
                        spare[:] = [7, 5, 6] if i < KQW else [7]
                        run_stages(nxtP, i)
                        if i < NS:
                            light = i < len(nxtP) and len(nxtP[i]) == 2
                            heavy = i < len(nxtP) and len(nxtP[i]) == 3
                            nd = 0 if heavy else (2 if light else 4)
                            o_open = i >= 2 and not tinfo(i - 1)[4] and not tinfo(i - 2)[1] == 0
                            if nd and o_open and DBG.get('dummy', 1):
                                fs = [I('matmul', ps[:, OB, :], zeros128[:], KT[:, 0:TT], start=False, stop=False) for _ in range(nd)]
                                P.op('pe', SEQ(*fs), reads=[f'KT{bs}.0', 'zeros128'], writes=[f'ps{OB}'])
                        if i < NS:
                            rec_p2(i)
                        if 1 <= i <= NS:
                            rec_p3(i - 1)
                        if i < NS:
                            rec_tri(i)
                        run_stages(nxt, i)
                        if 1 <= i <= NS:
                            rec_av(i - 1)
                    tile_n += NS
                    if kq_next:
                        assert not owned, owned
                        spare[:] = [7, 0, 1, 5, 6, 2, 3]
                        for ii in range(len(kq_next) + 6):
                            run_stages(kq_next, ii, order=(0, 1, 2, 3, 4, 5))
                        assert not owned, owned
                        spare[:] = [7]

                last = NC_ - 1 if DBG['nchunks'] == NC_ else None
                for tt in range(NTT):
                    if last is not None and tt < NTT - 1:
                        continue
                    for n in range(NC_):
                        b = next_bank()
                        dcs = [last] if last is not None else list(range(NC_))
                        mm_group(ps[:, b, :], [(Wout1v[:, dc, n * 128:(n + 1) * 128], y1[:, dc, tt * TT:(tt + 1) * TT]) for dc in dcs],
                                 f'ps{b}', [f'wout.{k}' for k in dcs] + [f'y1.{k}.{tt}' for k in dcs])
                        P.op('dve', I('tensor_tensor', xap(n, tt), ps[:, b, :], xap(n, tt), ALU.add),
                             reads=[f'ps{b}', xk(n, tt)], writes=[xk(n, tt)])
                B1 = dict(x1b=[szc2[0][:, cc * TT:(cc + 1) * TT] for cc in range(4)] + [spb[i][:, h, :] for i in range(2) for h in range(2)],
                          x1bk=[f'szc.{cc}' for cc in range(4)] + ['x1b.4', 'x1b.5', 'x1b.6', 'x1b.7'],
                          pst=[ebuf[0][:, k, :] for k in range(2)], pstk=['pst.0', 'pst.1'],
                          alias={'x1b.4': ['spb0'], 'x1b.5': ['spb0'], 'x1b.6': ['spb1'], 'x1b.7': ['spb1'],
                                 'pst.0': ['ebuf0'], 'pst.1': ['ebuf0'], 'pb2.0': ['gbuf0'], 'pb2.1': ['gbuf0']},
                          pb=[abuf[0][:, k, :] for k in range(2)], pbk=['abuf.0', 'abuf.1'],
                          x1b2=[KT2[1][:, cc * TT:(cc + 1) * TT] for cc in range(4)] + [QT2[1][:, cc * TT:(cc + 1) * TT] for cc in range(4)],
                          x1b2k=[f'KT1.{cc}' for cc in range(4)] + [f'QT1.{cc}' for cc in range(4)],
                          pb2=[gbuf[0].bitcast(BF16)[:, 1, k * TT:(k + 1) * TT] for k in range(2)],
                          pb2k=['pb2.0', 'pb2.1'],
                          sg=[lnv[0][:], lnv[1][:]], sgk=['lnv0', 'lnv1'],
                          tm=[gbuf[0][:, 0, :], gbuf[1][:, 0, :]], tmk=['gbuf0', 'gbuf1'],
                          wplek=['wple.0', 'wple.1'], wgatek=[f'wgate.{k}' for k in range(NC_)], store=True)
                ple_phase(1, Wple1v, Wgate1v, B1)
                P.emit()

    if n_layers < 2:
        for c in range(NC_):
            P.op('sp', I('dma_start', out=out_d[c * 128:(c + 1) * 128, :], in_=xT[:, c, :]),
                 reads=[xk(c, t) for t in range(NTT)], dma=True)
        P.emit()
    P.final_wait('sp')
    es.close()
    return P


def make_consts():
    c = np.zeros((128, NCB * 128), np.float32)
    idx = np.arange(128)
    c[:, C_ID * 128:(C_ID + 1) * 128] = np.eye(128, dtype=np.float32)
    c[:, C_TRI * 128:(C_TRI + 1) * 128] = (idx[:, None] >= idx[None, :]).astype(np.float32)
    c[:, C_BLK * 128:(C_BLK + 1) * 128] = ((idx[:, None] // 64) == (idx[None, :] // 64)).astype(np.float32) / 64.0
    c[:, C_OND * 128:(C_OND + 1) * 128] = 1.0 / D
    neg = (idx[:, None] >= idx[None, :]).astype(np.float32) * BIG
    c[:, C_NEGP * 128:(C_NEGP + 1) * 128] = neg
    c[:, C_NEGN * 128:(C_NEGN + 1) * 128] = -neg
    c[:, C_ONE * 128:(C_ONE + 1) * 128] = 1.0
    rc = np.zeros((128, 64), np.float32)
    for g, w in enumerate(POOL_W):
        t = np.arange(16)
        rc[:, g * 16:(g + 1) * 16] = 1.0 / np.minimum(t + 1, w).astype(np.float32)
    import ml_dtypes
    return c.astype(ml_dtypes.bfloat16), rc


_CONSTS = make_consts()


def prep_inputs(x, p, a_norm, a_w_in, a_w_group, a_scale, a_w_out, kv_norm, w_kv, k_norm,
                b_norm, b_w_in, b_q_norm, b_w_out, ple_w, ple_gate_w):
    f = lambda a: np.ascontiguousarray(np.asarray(a, dtype=np.float32))
    x = f(x); p = f(p)
    B = x.shape[0]
    vecs = np.zeros((128, NVEC), np.float32)
    col = lambda v: f(v).reshape(NC_, 128).T
    vecs[:, V_ANORM:V_ANORM + 8] = col(a_norm[0])
    vecs[:, V_ASCALE:V_ASCALE + 8] = col(a_scale[0])
    vecs[:, V_KVN:V_KVN + 8] = col(kv_norm)
    vecs[:, V_BN:V_BN + 8] = col(b_norm[0])
    vecs[:, V_KN] = np.tile(f(k_norm), 2)
    vecs[:, V_QN] = np.tile(f(b_q_norm[0]), 2)
    shared = {
        "vecs": vecs,
        "cst": _CONSTS[0],
        "rc": _CONSTS[1],
        "a_w_in": f(a_w_in[0]),
        "a_w_group": f(a_w_group[0]).reshape(D, 256),
        "a_w_out": f(a_w_out[0]),
        "w_kv_s": f(f(w_kv).reshape(D, 16, 128).transpose(1, 0, 2)),
        "b_w_in_s": f(f(b_w_in[0]).reshape(D, 16, 128).transpose(1, 0, 2)),
        "b_w_out": f(b_w_out[0]),
        "ple_w": f(ple_w),
        "ple_gate_w": f(ple_gate_w),
    }
    in_maps = []
    for b in range(B):
        m = dict(shared)
        m["xT"] = f(x[b].T)
        m["pT"] = f(p[:, b].transpose(0, 2, 1))
        in_maps.append(m)
    return in_maps


_CACHE = {}


def get_nc(n_layers=2):
    if n_layers not in _CACHE:
        nc = bass.Bass("TRN2", target_bir_lowering=False)
        build_program(nc, n_layers=n_layers)
        _CACHE[n_layers] = nc
    return _CACHE[n_layers]


def kernel(**inputs):
    in_maps = prep_inputs(**inputs)
    nc = get_nc(2)
    res = run_bass_kernel_spmd(nc, in_maps, core_ids=list(range(8)))
    out = np.stack([np.asarray(r["outT"], dtype=np.float32).T for r in res.results], axis=0)
    return np.ascontiguousarray(out)
```
